# Optimizing a Trainium2 kernel written in Bass

```python
import jax, jax.numpy as jnp
from jax import lax
import numpy as np

D_MODEL = 1024
BATCH = 4
SEQ = 4096
DEPTH = 4
DEC_BATCH = 16
DEC_SEQ = 2048
PAST_LEN = 128

PLE_DIM = 256
N_EVEN = (DEPTH + 1) // 2
N_ODD = DEPTH // 2
EPS = 1e-6
A_HEADS = 4
A_HD = 128
A_W = A_HEADS * A_HD
B_GROUPS = ((128, 1), (512, 4), (2048, 16))
N_B_GROUPS = 3
B_HEADS = 4
B_HD = 128
B_W = B_HEADS * B_HD
B_BLOCK = 64
ALIBI_MAX_EXP = 8.0
EV_IN = 5 * A_W + N_B_GROUPS * 3 * B_W
EV_OUT = A_W + B_W
C_HEADS = 4
C_KD = D_MODEL // 2
C_VD = D_MODEL
C_HDK = C_KD // C_HEADS
C_HDV = C_VD // C_HEADS
C_RANK = 16
GATE_NORMALIZER = 16.0
OD_IN = 2 * C_KD + 2 * C_VD + 2 * C_RANK
CHUNK = 64
SUB = 16
LOG_DECAY_MIN = -30.0
NEG_INF = -1e30
D_FF = 2816
CONV_W = 3

kernel_name = 'hybrid_bidir_hgrn2_dilattn_gla_encoder'


def _rmsnorm(x, g):
    x32 = x.astype(jnp.float32)
    y = x32 * lax.rsqrt(jnp.mean(x32 * x32, axis=-1, keepdims=True) + EPS)
    return y.astype(x.dtype) * g


def _head_rmsnorm(o, g):
    y = o * lax.rsqrt(jnp.mean(o * o, axis=-1, keepdims=True) + EPS)
    return y.reshape(o.shape[0], o.shape[1], -1) * g


def _chunk_gla(q, k, v, g):
    bsz, T, H, dk = q.shape
    dv = v.shape[-1]
    nc, ns = T // CHUNK, CHUNK // SUB
    f32 = jnp.float32

    def to_chunks(t):
        return t.astype(f32).reshape(bsz, nc, CHUNK, H, t.shape[-1]).transpose(1, 0, 3, 2, 4)

    g = jnp.maximum(g.astype(f32), LOG_DECAY_MIN)
    xs = (to_chunks(q), to_chunks(k), to_chunks(v), to_chunks(g))
    tri = jnp.tril(jnp.ones((SUB, SUB), dtype=bool))
    lower = jnp.tril(jnp.ones((ns, ns), dtype=bool), -1)

    def step(S, inp):
        qc, kc, vc, gc = inp
        b = jnp.cumsum(gc, axis=2)
        b_last = b[:, :, -1]
        o = jnp.einsum('bhcd,bhde->bhce', qc * jnp.exp(b), S)
        qs = qc.reshape(bsz, H, ns, SUB, dk)
        ks = kc.reshape(bsz, H, ns, SUB, dk)
        vs = vc.reshape(bsz, H, ns, SUB, dv)
        bs = b.reshape(bsz, H, ns, SUB, dk)
        r = bs[:, :, :, -1]
        q_off = qs[:, :, :, None] * jnp.exp(jnp.minimum(bs[:, :, :, None] - r[:, :, None, :, None, :], 0.0))
        k_off = ks * jnp.exp(r[:, :, :, None, :] - bs)
        a_off = jnp.where(lower[:, :, None, None],
                          jnp.einsum('bhijtd,bhjsd->bhijts', q_off, k_off), 0.0)
        w = jnp.where(tri[:, :, None],
                      jnp.exp(jnp.minimum(bs[:, :, :, :, None] - bs[:, :, :, None], 0.0)), 0.0)
        a_diag = jnp.einsum('bhitd,bhisd,bhitsd->bhits', qs, ks, w)
        o_intra = (jnp.einsum('bhijts,bhjse->bhite', a_off, vs)
                   + jnp.einsum('bhits,bhise->bhite', a_diag, vs))
        o = o + o_intra.reshape(bsz, H, CHUNK, dv)
        S = (jnp.exp(b_last)[..., None] * S
             + jnp.einsum('bhcd,bhce->bhde', kc * jnp.exp(b_last[:, :, None] - b), vc))
        return S, o

    S0 = jnp.zeros((bsz, H, dk, dv), f32)
    _, o = lax.scan(step, S0, xs)
    return o.transpose(1, 0, 3, 2, 4).reshape(bsz, T, H, dv)


def _bidir_gla(q, k_f, g_f, k_b, g_b, v):
    flip = lambda t: t[:, ::-1]
    o_f = _chunk_gla(q, k_f, v, g_f)
    o_b = flip(_chunk_gla(flip(q), flip(k_b), flip(v), flip(g_b)))
    return o_f + o_b


def _dilated_window_attn(q, k, v, dil, radius, slopes):
    bsz, T, H, dh = q.shape
    f32 = jnp.float32
    L = T // dil
    nb = -(-L // B_BLOCK)
    Lp = nb * B_BLOCK

    def split(t):
        return t.astype(f32).reshape(bsz, L, dil, H, dh).transpose(0, 2, 3, 1, 4)

    qb = jnp.pad(split(q) * (dh ** -0.5), ((0, 0), (0, 0), (0, 0), (0, Lp - L), (0, 0)))
    qb = qb.reshape(bsz, dil, H, nb, B_BLOCK, dh)
    pad_kv = ((0, 0), (0, 0), (0, 0), (B_BLOCK, Lp - L + B_BLOCK), (0, 0))
    kb = jnp.pad(split(k), pad_kv).reshape(bsz, dil, H, nb + 2, B_BLOCK, dh)
    vb = jnp.pad(split(v), pad_kv).reshape(bsz, dil, H, nb + 2, B_BLOCK, dh)
    s = jnp.concatenate([jnp.einsum('brhnad,brhncd->brhnac', qb, kb[:, :, :, i:i + nb])
                         for i in range(3)], axis=-1)
    rel = jnp.arange(3 * B_BLOCK)[None, :] - B_BLOCK - jnp.arange(B_BLOCK)[:, None]
    key_idx = jnp.arange(nb)[:, None] * B_BLOCK - B_BLOCK + jnp.arange(3 * B_BLOCK)[None, :]
    valid = (jnp.abs(rel) <= radius)[None] & ((key_idx >= 0) & (key_idx < L))[:, None, :]
    bias = -slopes.astype(f32)[:, None, None, None] * (dil * jnp.abs(rel)).astype(f32)
    s = jnp.where(valid, s + bias, NEG_INF)
    m = jnp.max(s, axis=-1, keepdims=True)
    pr = jnp.exp(s - m)
    den = jnp.sum(pr, axis=-1)
    o = jnp.einsum('brhnac,brhncd->brhnad', pr[..., :B_BLOCK], vb[:, :, :, 0:nb])
    for i in range(1, 3):
        o = o + jnp.einsum('brhnac,brhncd->brhnad', pr[..., i * B_BLOCK:(i + 1) * B_BLOCK],
                           vb[:, :, :, i:i + nb])
    o = o / den[..., None]
    lse = m[..., 0] + jnp.log(den)
    o = o.reshape(bsz, dil, H, Lp, dh)[:, :, :, :L].transpose(0, 3, 1, 2, 4).reshape(bsz, T, H, dh)
    lse = lse.reshape(bsz, dil, H, Lp)[..., :L].transpose(0, 3, 1, 2).reshape(bsz, T, H)
    return o, lse


def _alibi_slopes():
    n = N_B_GROUPS * B_HEADS
    e = jnp.arange(1, n + 1, dtype=jnp.float32)
    return (2.0 ** (-ALIBI_MAX_EXP * e / n)).reshape(N_B_GROUPS, B_HEADS)


def _hgrn_gates(z, lb):
    z = z.astype(jnp.float32)
    lb = lb.astype(jnp.float32).reshape(A_HEADS, A_HD)
    log_f = jnp.logaddexp(jnp.log(lb), jnp.log1p(-lb) + jax.nn.log_sigmoid(z))
    key = (1.0 - lb) * jax.nn.sigmoid(-z)
    return key, log_f


def _even_mixer(xn, w_in, lb_f, lb_b, a_norm_g, w_out):
    bsz, T, _ = xn.shape
    u = xn @ w_in
    a_q, a_zf, a_zb, a_i, a_g, b_qkv = jnp.split(u, [A_W, 2 * A_W, 3 * A_W, 4 * A_W, 5 * A_W], axis=-1)
    hd = lambda t: t.reshape(bsz, T, A_HEADS, A_HD)
    k_f, g_f = _hgrn_gates(hd(a_zf), lb_f)
    k_b, g_b = _hgrn_gates(hd(a_zb), lb_b)
    o_a = _bidir_gla(hd(a_q), k_f, g_f, k_b, g_b, hd(a_i))
    out_a = _head_rmsnorm(o_a, a_norm_g) * jax.nn.silu(a_g.astype(jnp.float32))
    b = b_qkv.reshape(bsz, T, N_B_GROUPS, 3, B_HEADS, B_HD)
    slopes = _alibi_slopes()
    outs, lses = [], []
    for gi, (win, dil) in enumerate(B_GROUPS):
        o_g, lse_g = _dilated_window_attn(b[:, :, gi, 0], b[:, :, gi, 1], b[:, :, gi, 2],
                                          dil, win // (2 * dil), slopes[gi])
        outs.append(o_g)
        lses.append(lse_g)
    alpha = jax.nn.softmax(jnp.stack(lses, axis=0), axis=0)
    out_b = jnp.sum(alpha[..., None] * jnp.stack(outs, axis=0), axis=0).reshape(bsz, T, B_W)
    mixed = jnp.concatenate([out_a, out_b], axis=-1).astype(xn.dtype)
    return mixed @ w_out


def _odd_mixer(xn, w_in, w_gate_up, b_gate, norm_g, w_out):
    bsz, T, _ = xn.shape
    u = xn @ w_in
    q, k, v, g, lr_f, lr_b = jnp.split(
        u, [C_KD, 2 * C_KD, 2 * C_KD + C_VD, 2 * C_KD + 2 * C_VD, 2 * C_KD + 2 * C_VD + C_RANK], axis=-1)
    hk = lambda t: t.reshape(bsz, T, C_HEADS, C_HDK)

    def decay(lr, d):
        logit = (lr @ w_gate_up[d] + b_gate[d]).astype(jnp.float32)
        return hk(jax.nn.log_sigmoid(logit) / GATE_NORMALIZER)

    qh = hk(q.astype(jnp.float32) * (C_HDK ** -0.5))
    kh = hk(k)
    vh = v.reshape(bsz, T, C_HEADS, C_HDV)
    o = _bidir_gla(qh, kh, decay(lr_f, 0), kh, decay(lr_b, 1), vh)
    out = _head_rmsnorm(o, norm_g) * jax.nn.silu(g.astype(jnp.float32))
    return out.astype(xn.dtype) @ w_out


def _conv_ffn(x, w_up, conv_w, conv_b, w_down):
    u = x @ w_up
    up = jnp.pad(u, ((0, 0), (1, 1), (0, 0)))
    u = up[:, :-2] * conv_w[0] + up[:, 1:-1] * conv_w[1] + up[:, 2:] * conv_w[2] + conv_b
    a, gate = jnp.split(u, 2, axis=-1)
    return (a * jax.nn.gelu(gate, approximate=False)) @ w_down


def _trunk(x, p, norm_mix_g, ev_w_in, hgrn_lb, hgrn_norm_g, ev_w_out, od_w_in, gla_w_gate_up,
           gla_b_gate, gla_norm_g, od_w_out, norm_ffn_g, ffn_w_up, ffn_conv_w, ffn_conv_b, ffn_w_down,
           norm_ple_g, ple_w_gate, ple_w_proj, norm_out_g):
    h = x
    for l in range(DEPTH):
        xn = _rmsnorm(h, norm_mix_g[l])
        if l % 2 == 0:
            e = l // 2
            h = h + _even_mixer(xn, ev_w_in[e], hgrn_lb[0, e], hgrn_lb[1, e], hgrn_norm_g[e], ev_w_out[e])
        else:
            o = l // 2
            h = h + _odd_mixer(xn, od_w_in[o], gla_w_gate_up[o], gla_b_gate[o], gla_norm_g[o], od_w_out[o])
        h = h + _conv_ffn(_rmsnorm(h, norm_ffn_g[l]), ffn_w_up[l], ffn_conv_w[l], ffn_conv_b[l], ffn_w_down[l])
        gate = jax.nn.sigmoid(_rmsnorm(h, norm_ple_g[l]) @ ple_w_gate[l])
        h = h + gate * (p[l] @ ple_w_proj[l])
    return _rmsnorm(h, norm_out_g)


def setup_inputs(seed: int = 0) -> dict:
    key = jax.random.key(seed)
    ks = iter(jax.random.split(key, 40))
    f32 = jnp.float32
    nrm = lambda shape, scale: jax.random.normal(next(ks), shape, f32) * scale
    gain = lambda shape: 1.0 + 0.05 * jax.random.normal(next(ks), shape, f32)
    centre = jnp.array([0.0, 1.0, 0.0], f32)[None, :, None]
    return {
        'x_prompt': nrm((BATCH, SEQ, D_MODEL), 1.0),
        'x_sample': nrm((DEC_BATCH, DEC_SEQ, D_MODEL), 1.0),
        'p_prompt': nrm((DEPTH, BATCH, SEQ, PLE_DIM), 1.0),
        'p_sample': nrm((DEPTH, DEC_BATCH, DEC_SEQ, PLE_DIM), 1.0),
        'norm_mix_g': gain((DEPTH, D_MODEL)),
        'ev_w_in': nrm((N_EVEN, D_MODEL, EV_IN), D_MODEL ** -0.5),
        'hgrn_lb_logits': nrm((2, N_EVEN, A_W), 1.0),
        'hgrn_norm_g': gain((N_EVEN, A_W)),
        'ev_w_out': nrm((N_EVEN, EV_OUT, D_MODEL), EV_OUT ** -0.5),
        'od_w_in': nrm((N_ODD, D_MODEL, OD_IN), D_MODEL ** -0.5),
        'gla_w_gate_up': nrm((N_ODD, 2, C_RANK, C_KD), C_RANK ** -0.5),
        'gla_b_gate': nrm((N_ODD, 2, C_KD), 0.1),
        'gla_norm_g': gain((N_ODD, C_VD)),
        'od_w_out': nrm((N_ODD, C_VD, D_MODEL), C_VD ** -0.5),
        'norm_ffn_g': gain((DEPTH, D_MODEL)),
        'ffn_w_up': nrm((DEPTH, D_MODEL, 2 * D_FF), D_MODEL ** -0.5),
        'ffn_conv_w': centre + nrm((DEPTH, CONV_W, 2 * D_FF), 0.3),
        'ffn_conv_b': nrm((DEPTH, 2 * D_FF), 0.02),
        'ffn_w_down': nrm((DEPTH, D_FF, D_MODEL), D_FF ** -0.5),
        'norm_ple_g': gain((DEPTH, D_MODEL)),
        'ple_w_gate': nrm((DEPTH, D_MODEL, D_MODEL), D_MODEL ** -0.5),
        'ple_w_proj': nrm((DEPTH, PLE_DIM, D_MODEL), PLE_DIM ** -0.5),
        'norm_out_g': gain((D_MODEL,)),
    }


def reference(x_prompt, x_sample, p_prompt, p_sample, norm_mix_g, ev_w_in, hgrn_lb_logits, hgrn_norm_g,
              ev_w_out, od_w_in, gla_w_gate_up, gla_b_gate, gla_norm_g, od_w_out, norm_ffn_g, ffn_w_up,
              ffn_conv_w, ffn_conv_b, ffn_w_down, norm_ple_g, ple_w_gate, ple_w_proj, norm_out_g):
    lb = jnp.cumsum(jax.nn.softmax(hgrn_lb_logits.astype(jnp.float32), axis=1), axis=1)
    lb = lb - lb[:, :1]
    y_prompt = _trunk(x_prompt, p_prompt, norm_mix_g, ev_w_in, lb, hgrn_norm_g, ev_w_out, od_w_in,
                      gla_w_gate_up, gla_b_gate, gla_norm_g, od_w_out, norm_ffn_g, ffn_w_up, ffn_conv_w,
                      ffn_conv_b, ffn_w_down, norm_ple_g, ple_w_gate, ple_w_proj, norm_out_g)
    y_sample = _trunk(x_sample, p_sample, norm_mix_g, ev_w_in, lb, hgrn_norm_g, ev_w_out, od_w_in,
                      gla_w_gate_up, gla_b_gate, gla_norm_g, od_w_out, norm_ffn_g, ffn_w_up, ffn_conv_w,
                      ffn_conv_b, ffn_w_down, norm_ple_g, ple_w_gate, ple_w_proj, norm_out_g)
    return (y_prompt, y_sample)
```

```python
import contextlib
import numpy as np
import concourse.bass as bass
import concourse.mybir as mybir
from concourse.bass_utils import run_bass_kernel_spmd

F32 = mybir.dt.float32
BF16 = mybir.dt.bfloat16
AF = mybir.ActivationFunctionType
ALU = mybir.AluOpType
AX = mybir.AxisListType

D = 1024
DC = 8
DEPTH = 4
PLE = 256
EPS = 1e-6
D_FF = 2816
EV_IN = 7168
OD_IN = 3104


class Tr:
    __slots__ = ("w", "r")

    def __init__(self):
        self.w = {}
        self.r = {}


class Buf:
    __slots__ = ("t", "leaves", "name", "grid")

    def __init__(self, t, name="", leaves=None):
        self.t = t
        self.leaves = leaves if leaves is not None else [Tr()]
        self.name = name
        self.grid = None

    def __getitem__(self, idx):
        return self.t[idx]

    def ap(self):
        return self.t.ap()

    def make_grid(self, n0, n1, blk):
        self.grid = (n0, n1, blk, [[Tr() for _ in range(n1)] for _ in range(n0)])
        self.leaves = [l for row in self.grid[3] for l in row]
        return self

    def sel(self, rows, t0, t1):
        n0, n1, blk, g = self.grid
        b0, b1 = t0 // blk, (t1 - 1) // blk
        lv = []
        for r in rows:
            lv.extend(g[r][b0:b1 + 1])
        return Buf(self.t, self.name, lv)

    def part(self, i):
        return Buf(self.t, self.name, [self.leaves[i]])

    def with_parts(self, n):
        self.leaves = [Tr() for _ in range(n)]
        return self


class K:
    def __init__(self, nc, es, n_dma_sems=64):
        self.nc = nc
        self.es = es
        self.eng = {"pe": nc.tensor, "act": nc.scalar, "dve": nc.vector, "pool": nc.gpsimd, "sp": nc.sync}
        self.cnt = {}
        self.seen = {e: {} for e in self.eng}
        self.semh = {}
        for e in self.eng:
            s = es.enter_context(nc.semaphore("c_" + e))
            self.semh[e] = s
            self.cnt[e] = 0
        self.dsem = []
        self.dtot = []
        for i in range(n_dma_sems):
            self.dsem.append(es.enter_context(nc.semaphore("d%d" % i)))
            self.dtot.append(0)
            self.semh[("d", i)] = self.dsem[i]
        self.dnext = 0
        self.dnext_sw = 0
        self.n_hw = n_dma_sems - 16
        self.n_ins = 0

    def sb(self, name, shape, dt):
        self.uid = getattr(self, "uid", 0) + 1
        name = "%s_u%d" % (name, self.uid)
        return Buf(self.es.enter_context(self.nc.sbuf_tensor(name, list(shape), dt)), name)

    def ps(self, name, shape, dt=F32):
        return Buf(self.es.enter_context(self.nc.psum_tensor(name, list(shape), dt)), name)

    def dram(self, name, shape, dt, kind="Internal"):
        return Buf(self.nc.dram_tensor(name, list(shape), dt, kind=kind), name)

    def _need(self, E, reads, writes):
        need = {}
        for b in reads:
            for l in b.leaves:
                for k, v in l.w.items():
                    if need.get(k, 0) < v:
                        need[k] = v
        for b in writes:
            for l in b.leaves:
                for k, v in l.w.items():
                    if need.get(k, 0) < v:
                        need[k] = v
                for k, v in l.r.items():
                    if need.get(k, 0) < v:
                        need[k] = v
        seen = self.seen[E]
        eng = self.eng[E]
        for k, v in need.items():
            if k == "pe" and E == "pe":
                continue
            if seen.get(k, 0) < v:
                eng.wait_ge(self.semh[k], v)
                seen[k] = v

    def _mark(self, tok, reads, writes):
        k, v = tok
        for b in reads:
            for l in b.leaves:
                if l.r.get(k, 0) < v:
                    l.r[k] = v
        for b in writes:
            for l in b.leaves:
                l.w = {k: v}
                l.r = {}

    def op(self, E, reads, writes, fn):
        self._need(E, reads, writes)
        ins = fn(self.eng[E])
        self.cnt[E] += 1
        ins.then_inc(self.semh[E], 1)
        self._mark((E, self.cnt[E]), reads, writes)
        self.n_ins += 1
        return ins

    def dma(self, out_ap, in_ap, reads, writes, q="sp", **kw):
        if q == "pool":
            i = self.n_hw + self.dnext_sw
            self.dnext_sw = (self.dnext_sw + 1) % (len(self.dsem) - self.n_hw)
        else:
            i = self.dnext
            self.dnext = (self.dnext + 1) % self.n_hw
        key = ("d", i)
        self._need(q, reads, writes)
        if self.dtot[i] > 0 and self.seen[q].get(key, 0) < self.dtot[i]:
            self.eng[q].wait_ge(self.dsem[i], self.dtot[i])
            self.seen[q][key] = self.dtot[i]
        ins = self.eng[q].dma_start(out=out_ap, in_=in_ap, **kw)
        self.dtot[i] += 16
        ins.then_inc(self.dsem[i], 16)
        self._mark((key, self.dtot[i]), reads, writes)
        self.n_ins += 1
        return ins

    def finish(self, bufs, q="sp"):
        for b in bufs:
            for l in b.leaves:
                for k, v in l.w.items():
                    if self.seen[q].get(k, 0) < v:
                        self.eng[q].wait_ge(self.semh[k], v)
                        self.seen[q][k] = v


class Ring:
    def __init__(self, bufs):
        self.bufs = bufs
        self.i = 0

    def next(self):
        b = self.bufs[self.i]
        self.i = (self.i + 1) % len(self.bufs)
        return b


def sb_ring(k, name, shape, dt, n):
    return Ring([k.sb("%s%d" % (name, i), shape, dt) for i in range(n)])


class Cfg:
    def __init__(self, seqs, n_layers=DEPTH, stop_after=None, debug=False, links=None):
        self.seqs = list(seqs)
        self.nt = sum(seqs)
        self.n_layers = n_layers
        self.starts = [sum(seqs[:i]) for i in range(len(seqs))]
        self.stop_after = stop_after
        self.debug = debug
        self.parts = ("gla", "attn")
        self.links = dict(links or {})


PARAM_SHAPES = [
    ("norm_mix_g", [DEPTH, D]), ("ev_w_in", [2, D, EV_IN]), ("hgrn_lb_logits", [2, 2, 512]),
    ("hgrn_norm_g", [2, 512]), ("ev_w_out", [2, D, D]), ("od_w_in", [2, D, OD_IN]),
    ("gla_w_gate_up", [2, 2, 16, 512]), ("gla_b_gate", [2, 2, 512]), ("gla_norm_g", [2, D]),
    ("od_w_out", [2, D, D]), ("norm_ffn_g", [DEPTH, D]), ("ffn_w_up", [DEPTH, D, 2 * D_FF]),
    ("ffn_conv_w", [DEPTH, 3, 2 * D_FF]), ("ffn_conv_b", [DEPTH, 2 * D_FF]),
    ("ffn_w_down", [DEPTH, D_FF, D]), ("norm_ple_g", [DEPTH, D]), ("ple_w_gate", [DEPTH, D, D]),
    ("ple_w_proj", [DEPTH, PLE, D]), ("norm_out_g", [D]),
]
BIG_W = ["ev_w_in", "ev_w_out", "od_w_in", "od_w_out", "ffn_w_up", "ffn_w_down", "ple_w_gate", "ple_w_proj"]
B_GROUPS_DIL = (1, 4, 16)
NFF = D_FF // 128
NEG = -1.0e30


class Ctx:
    pass


def subtiles(n):
    out = []
    a = 0
    while a < n:
        w = min(512, n - a)
        out.append((a, w))
        a += w
    return out


def build(cfg):
    nc = bass.Bass("TRN2", target_bir_lowering=False)
    NT = cfg.nt
    es = contextlib.ExitStack()
    with es:
        k = K(nc, es)
        c = Ctx()
        c.k, c.cfg, c.nc = k, cfg, nc
        c.ext = {}
        c.ext["x"] = k.dram("x", [NT, D], F32, kind="ExternalInput")
        c.ext["p"] = k.dram("p", [DEPTH, NT, PLE], F32, kind="ExternalInput")
        c.ext["link"] = k.dram("link", [128, 1], F32, kind="ExternalInput")
        for name, shape in PARAM_SHAPES:
            c.ext[name] = k.dram(name, shape, F32, kind="ExternalInput")
        c.y = k.dram("y", [NT, D], F32, kind="ExternalOutput")
        nb128 = NT // 128
        c.h_d = k.dram("h_scr", [DC, 128, NT], F32).make_grid(1, nb128, 128)
        c.h_n = k.dram("h_scr2", [DC, 128, NT], F32).make_grid(1, nb128, 128)
        c.ufm = k.dram("ufm_scr", [40, 128, NT], BF16).make_grid(40, nb128, 128)
        c.gfm = k.dram("gfm_scr", [8, 128, NT], F32).make_grid(8, nb128, 128)
        c.utm = k.dram("utm_scr", [4, NT, 512], BF16).make_grid(4, nb128, 128)
        c.mix = k.dram("mix_scr", [8, 128, NT], BF16).make_grid(8, nb128, 128)
        c.wb = {}
        for name in BIG_W:
            shp = dict(PARAM_SHAPES)[name]
            c.wb[name] = k.dram("wb_" + name, [shp[0] * shp[1], shp[2]], BF16).with_parts(shp[0])

        c.ps2 = Ring([k.ps("ps2_%d" % i, [128, 1024]) for i in range(2)])
        c.ps1 = Ring([(k.ps("ps1_%d" % i, [128, 512]), 0) for i in range(3)])
        psT = k.ps("psT", [128, 1024], BF16).with_parts(2)
        c.psT = Ring([psT])

        setup_constants(c)
        if cfg.debug:
            with Scope(c):
                z = k.sb("dbg_zero", [128, NT], BF16)
                k.op("pool", [], [z], lambda e: e.memset(z[:], 0.0))
                for i in range(8):
                    k.dma(c.mix.ap()[i], z[:], [z], [c.mix])
        precompute_mixer_constants(c)
        cast_weights(c)
        phase0(c)
        done = False
        for l in range(cfg.n_layers):
            for ph in ("A", "B", "C"):
                if ph == "A":
                    phase_A(c, l)
                elif ph == "B":
                    phase_B(c, l)
                else:
                    phase_C(c, l)
                if cfg.stop_after == (ph, l):
                    done = True
                    break
            if done:
                break
        final_norm(c)
        c.dbg = []
        if cfg.debug:
            add_debug_outputs(c)
        k.finish([c.y] + c.dbg)
    c.n_ins = k.n_ins
    print('[kernel] instructions:', k.n_ins, 'per-engine:', dict(k.cnt))
    return nc


def ps1_tile(c):
    b, off = c.ps1.next()
    return b, off


def setup_constants(c):
    k = c.k
    c.ident_f = k.sb("ident_f", [128, 128], F32)
    c.ident_b = k.sb("ident_b", [128, 128], BF16)
    c.ones_b = k.sb("ones_b", [128, 128], BF16)
    k.op("pool", [], [c.ident_f], lambda e: e.memset(c.ident_f[:], 0.0))
    k.op("pool", [c.ident_f], [c.ident_f], lambda e: e.affine_select(
        out=c.ident_f[:], in_=c.ident_f[:], pattern=[[-1, 128]], compare_op=ALU.not_equal,
        fill=1.0, base=0, channel_multiplier=1))
    k.op("pool", [c.ident_f], [c.ident_b], lambda e: e.tensor_copy(out=c.ident_b[:], in_=c.ident_f[:]))
    k.op("pool", [], [c.ones_b], lambda e: e.memset(c.ones_b[:], 1.0))

    def load_small(name, pattern, ncol, **kw):
        t = k.sb("p_" + name, [128, ncol], F32)
        k.dma(t[:], c.ext[name].ap().rearrange(pattern, p=128, **kw), [c.ext[name]], [t],
              allow_slow_non_contiguous=True)
        return t

    c.g_mix = load_small("norm_mix_g", "l (c p) -> p (l c)", DEPTH * DC)
    c.g_ffn = load_small("norm_ffn_g", "l (c p) -> p (l c)", DEPTH * DC)
    c.g_ple = load_small("norm_ple_g", "l (c p) -> p (l c)", DEPTH * DC)
    c.g_out = load_small("norm_out_g", "(c p) -> p c", DC)
    c.lbl = load_small("hgrn_lb_logits", "d e (c p) -> p (d e c)", 16)
    c.g_hg = load_small("hgrn_norm_g", "e (c p) -> p (e c)", 8)
    c.b_gate = load_small("gla_b_gate", "o d (c p) -> p (o d c)", 16)
    c.g_gla = load_small("gla_norm_g", "o (c p) -> p (o c)", 16)
    c.conv_w = load_small("ffn_conv_w", "l j (c p) -> p (l j c)", DEPTH * 3 * 2 * NFF)
    c.conv_b = load_small("ffn_conv_b", "l (c p) -> p (l c)", DEPTH * 2 * NFF)
    c.link = k.sb("link_sb", [128, 1], F32)
    k.dma(c.link[:], c.ext["link"].ap(), [c.ext["link"]], [c.link])
    c.conv_w_nl = k.sb("conv_w_nl", [128, DEPTH * 3 * 2 * NFF], F32)
    lm1 = k.sb("link_m1", [128, 1], F32)
    k.op("dve", [c.link], [lm1], lambda e: e.tensor_scalar(
        out=lm1[:], in0=c.link[:], scalar1=-1.0, scalar2=None, op0=ALU.add))
    k.op("dve", [c.conv_w, lm1], [c.conv_w_nl], lambda e: e.tensor_scalar(
        out=c.conv_w_nl[:], in0=c.conv_w[:], scalar1=lm1[:, 0:1], scalar2=None, op0=ALU.mult))
    c.nb_gate = k.sb("nb_gate", [128, 16], F32)
    k.op("dve", [c.b_gate], [c.nb_gate], lambda e: e.tensor_scalar(
        out=c.nb_gate[:], in0=c.b_gate[:], scalar1=-1.0, scalar2=None, op0=ALU.mult))
    c.lb = k.sb("lb", [128, 16], F32)
    c.oml = k.sb("oml", [128, 16], F32)
    c.noml = k.sb("noml", [128, 16], F32)
    k.op("pool", [], [c.lb], lambda e: e.memset(c.lb[:], 0.0))
    lb4 = c.lb[:].rearrange("p (d e c) -> p d e c", d=2, e=2)
    ll4 = c.lbl[:].rearrange("p (d e c) -> p d e c", d=2, e=2)
    k.op("dve", [c.lbl, c.lb], [c.lb], lambda e: e.tensor_tensor(
        out=lb4[:, :, 1, :], in0=ll4[:, :, 1, :], in1=ll4[:, :, 0, :], op=ALU.subtract))
    k.op("act", [c.lb], [c.lb], lambda e: e.activation(out=lb4[:, :, 1, :], in_=lb4[:, :, 1, :], func=AF.Sigmoid))
    k.op("dve", [c.lb], [c.oml], lambda e: e.tensor_scalar(
        out=c.oml[:], in0=c.lb[:], scalar1=-1.0, scalar2=1.0, op0=ALU.mult, op1=ALU.add))
    k.op("dve", [c.oml], [c.noml], lambda e: e.tensor_scalar(
        out=c.noml[:], in0=c.oml[:], scalar1=-1.0, scalar2=None, op0=ALU.mult))
    wgu_f = k.sb("wgu_f", [16, 4, 512], F32)
    k.dma(wgu_f[:], c.ext["gla_w_gate_up"].ap().rearrange("o d r c -> r (o d) c"), [c.ext["gla_w_gate_up"]], [wgu_f])
    c.wgu = k.sb("wgu_b", [16, 4, 512], BF16)
    k.op("dve", [wgu_f], [c.wgu], lambda e: e.tensor_copy(out=c.wgu[:], in_=wgu_f[:]))


def _gla_constants_compute(c):
    k = c.k
    c.rmask = k.sb("rmask", [128, 512], F32)
    k.op("pool", [], [c.rmask], lambda e: e.memset(c.rmask[:], 1.0))
    k.op("pool", [c.rmask], [c.rmask], lambda e: e.memset(
        c.rmask[:].rearrange("p (c j) -> p c j", j=64)[:, :, 0:1], 0.0))
    c.maskF = k.sb("maskF", [128, 128], F32)
    c.maskB = k.sb("maskB", [128, 128], F32)
    for m, sgn, (r0, c0) in ((c.maskF, 1, (0, 64)), (c.maskB, -1, (64, 0))):
        k.op("pool", [], [m], lambda e, m=m: e.memset(m[:], 1.0))
        k.op("pool", [m], [m], lambda e, m=m, sgn=sgn: e.affine_select(
            out=m[:], in_=m[:], pattern=[[sgn, 128]], compare_op=ALU.is_ge, fill=0.0, base=0,
            channel_multiplier=-sgn))
        k.op("pool", [m], [m], lambda e, m=m, r0=r0, c0=c0: e.memset(m[r0:r0 + 64, c0:c0 + 64], 0.0))


def _attn_constants_compute(c):
    k = c.k
    c.bias_hi = k.sb("bias_hi", [128, 12, 256], BF16)
    c.bias_lo = k.sb("bias_lo", [128, 12, 256], BF16)
    rel = k.sb("rel_f", [128, 256], F32)
    tmpb = k.sb("tmp_bias", [128, 256], F32)
    tmph = k.sb("tmp_biash", [128, 256], F32)
    k.op("pool", [], [rel], lambda e: e.iota(rel[:], pattern=[[-1, 256]], base=64, channel_multiplier=1,
                                              allow_small_or_imprecise_dtypes=True))
    k.op("act", [rel], [rel], lambda e: e.activation(out=rel[:], in_=rel[:], func=AF.Abs))
    for g in range(3):
        for h in range(4):
            idx = g * 4 + h
            slope = 2.0 ** (-8.0 * (idx + 1) / 12.0)
            coef = -slope * B_GROUPS_DIL[g]
            k.op("dve", [rel], [tmpb], lambda e, coef=coef: e.tensor_scalar(
                out=tmpb[:], in0=rel[:], scalar1=coef, scalar2=None, op0=ALU.mult))
            k.op("pool", [tmpb], [tmpb], lambda e: e.affine_select(
                out=tmpb[:], in_=tmpb[:], pattern=[[1, 256]], compare_op=ALU.is_ge, fill=NEG, base=0,
                channel_multiplier=-1))
            k.op("pool", [tmpb], [tmpb], lambda e: e.affine_select(
                out=tmpb[:], in_=tmpb[:], pattern=[[-1, 256]], compare_op=ALU.is_ge, fill=NEG, base=128,
                channel_multiplier=1))
            k.op("dve", [tmpb], [c.bias_hi], lambda e, idx=idx: e.tensor_copy(out=c.bias_hi[:, idx, :], in_=tmpb[:]))
            k.op("dve", [tmpb, c.bias_hi], [tmph], lambda e, idx=idx: e.tensor_tensor(
                out=tmph[:], in0=tmpb[:], in1=c.bias_hi[:, idx, :], op=ALU.subtract))
            k.op("dve", [tmph], [c.bias_lo], lambda e, idx=idx: e.tensor_copy(out=c.bias_lo[:, idx, :], in_=tmph[:]))
    c.bias0_hi = k.sb("bias0_hi", [64, 12, 128], BF16)
    c.bias0_lo = k.sb("bias0_lo", [64, 12, 128], BF16)
    k.dma(c.bias0_hi[:], c.bias_hi[64:128, :, 128:256], [c.bias_hi], [c.bias0_hi])
    k.dma(c.bias0_lo[:], c.bias_lo[64:128, :, 128:256], [c.bias_lo], [c.bias0_lo])


_GLA_CONSTS = (("rmask", [128, 512], F32), ("maskF", [128, 128], F32), ("maskB", [128, 128], F32))
_ATTN_CONSTS = (("bias_hi", [128, 12, 256], BF16), ("bias_lo", [128, 12, 256], BF16),
                ("bias0_hi", [64, 12, 128], BF16), ("bias0_lo", [64, 12, 128], BF16))


def precompute_mixer_constants(c):
    k = c.k
    c.cst = {}
    with Scope(c):
        _gla_constants_compute(c)
        _attn_constants_compute(c)
        for name, shape, dt in _GLA_CONSTS + _ATTN_CONSTS:
            d = k.dram("cst_" + name, shape, dt)
            k.dma(d.ap(), getattr(c, name)[:], [getattr(c, name)], [d])
            c.cst[name] = d


def _load_consts(c, specs):
    k = c.k
    for name, shape, dt in specs:
        t = k.sb(name, shape, dt)
        k.dma(t[:], c.cst[name].ap(), [c.cst[name]], [t])
        setattr(c, name, t)


def gla_constants(c):
    _load_consts(c, _GLA_CONSTS)


def attn_constants(c):
    _load_consts(c, _ATTN_CONSTS)


def barrier(c):
    k = c.k
    for E in k.eng:
        for E2 in k.eng:
            if E2 != E and k.cnt[E2] > 0 and k.seen[E].get(E2, 0) < k.cnt[E2]:
                k.eng[E].wait_ge(k.semh[E2], k.cnt[E2])
                k.seen[E][E2] = k.cnt[E2]
        for i, tot in enumerate(k.dtot[:k.n_hw]):
            key = ("d", i)
            if tot > 0 and k.seen[E].get(key, 0) < tot:
                k.eng[E].wait_ge(k.dsem[i], tot)
                k.seen[E][key] = tot


class Scope:
    def __init__(self, c):
        self.c = c

    def __enter__(self):
        self.saved = self.c.k.es
        self.es = contextlib.ExitStack()
        self.es.__enter__()
        self.c.k.es = self.es
        return self

    def __exit__(self, *a):
        barrier(self.c)
        self.c.k.es = self.saved
        return self.es.__exit__(*a)


def cast_weights(c):
    k = c.k
    order = []
    for l in range(DEPTH):
        e = l // 2
        order += [("ev_w_in", e), ("ev_w_out", e)] if l % 2 == 0 else [("od_w_in", e), ("od_w_out", e)]
        order += [("ffn_w_up", l), ("ffn_w_down", l), ("ple_w_gate", l), ("ple_w_proj", l)]
    for name, li in order:
        shp = dict(PARAM_SHAPES)[name]
        rows = shp[1]
        src = c.ext[name].ap()
        dst = c.wb[name].ap()
        RB = 256
        for r0 in range(0, rows, RB):
            r1 = min(rows, r0 + RB)
            k.dma(dst[li * rows + r0:li * rows + r1, :], src[li, r0:r1, :], [c.ext[name]], [c.wb[name].part(li)],
                  q="pool")


def phase0(c):
    k = c.k
    NT = c.cfg.nt
    with Scope(c):
        xin_r = sb_ring(k, "xin", [128, D], F32, 3)
        hT_r = sb_ring(k, "hT", [128, DC, 128], F32, 3)
        x_ap = c.ext["x"].ap()
        h_ap = c.h_d.ap().rearrange("c p t -> p c t")
        for b in range(NT // 128):
            xt = xin_r.next()
            k.dma(xt[:], x_ap[b * 128:(b + 1) * 128, :], [c.ext["x"]], [xt])
            ht = hT_r.next()
            for half in range(2):
                p, off = ps1_tile(c)
                for j in range(4):
                    cc = half * 4 + j
                    k.op("pe", [xt, c.ident_f], [p], lambda e, cc=cc, j=j, p=p, off=off, xt=xt: e.transpose(
                        p[:, off + j * 128:off + (j + 1) * 128], xt[:, cc * 128:(cc + 1) * 128], c.ident_f[:]))
                src = p[:, off:off + 512].rearrange("p (c t) -> p c t", c=4)
                if half == 0:
                    k.op("act", [p], [ht], lambda e, src=src, ht=ht: e.copy(out=ht[:, 0:4, :], in_=src))
                else:
                    k.op("dve", [p, ht], [ht], lambda e, src=src, ht=ht: e.tensor_copy(out=ht[:, 4:8, :], in_=src))
            k.dma(h_ap[:, :, b * 128:(b + 1) * 128], ht[:], [ht], [c.h_d.sel([0], b * 128, (b + 1) * 128)], q="act")


def rms_rstd(c, hT, n, sq_r, rstd_r, nchunks=DC, dim=D):
    k = c.k
    sq = sq_r.next()
    k.op("act", [hT], [sq], lambda e: e.activation(out=sq[:, :nchunks, :n], in_=hT[:, :nchunks, :n], func=AF.Square))
    rstd = rstd_r.next()
    for (a, w) in subtiles(n):
        p, off = ps1_tile(c)
        for cc in range(nchunks):
            k.op("pe", [sq, c.ones_b], [p], lambda e, cc=cc, p=p, off=off, a=a, w=w: e.matmul(
                p[:, off:off + w], lhsT=c.ones_b[:], rhs=sq[:, cc, a:a + w], start=(cc == 0), stop=(cc == nchunks - 1)))
        k.op("act", [p], [rstd], lambda e, p=p, off=off, a=a, w=w: e.activation(
            out=rstd[:, a:a + w], in_=p[:, off:off + w], func=AF.Ln, scale=1.0 / dim, bias=EPS))
    k.op("act", [rstd], [rstd], lambda e: e.activation(out=rstd[:, :n], in_=rstd[:, :n], func=AF.Exp, scale=-0.5))
    return rstd


def norm_xn(c, hT, n, gcols, sq_r, rstd_r, xn):
    k = c.k
    rstd = rms_rstd(c, hT, n, sq_r, rstd_r)
    for cc in range(DC):
        k.op("dve", [hT, rstd], [xn], lambda e, cc=cc: e.scalar_tensor_tensor(
            out=xn[:, cc, :n], in0=hT[:, cc, :n], scalar=gcols[:, cc:cc + 1], in1=rstd[:, :n],
            op0=ALU.mult, op1=ALU.mult))
    return xn


def final_norm(c):
    k = c.k
    NT = c.cfg.nt
    TT = 512
    with Scope(c):
        hin_r = sb_ring(k, "f_hin", [128, DC, TT], F32, 2)
        sq_r = sb_ring(k, "f_sq", [128, DC, TT], BF16, 2)
        rstd_r = sb_ring(k, "f_rstd", [128, TT], F32, 2)
        xn_r = sb_ring(k, "f_xn", [128, DC, TT], F32, 2)
        yo_r = sb_ring(k, "f_yo", [128, D], F32, 3)
        h_ap = c.h_d.ap().rearrange("c p t -> p c t")
        y_ap = c.y.ap()
        for t0 in range(0, NT, TT):
            n = min(TT, NT - t0)
            ht = hin_r.next()
            k.dma(ht[:, :, :n], h_ap[:, :, t0:t0 + n], [c.h_d.sel([0], t0, t0 + n)], [ht])
            rstd = rms_rstd(c, ht, n, sq_r, rstd_r)
            xn = xn_r.next()
            for cc in range(DC):
                k.op("dve", [ht, rstd], [xn], lambda e, cc=cc: e.scalar_tensor_tensor(
                    out=xn[:, cc, :n], in0=ht[:, cc, :n], scalar=c.g_out[:, cc:cc + 1], in1=rstd[:, :n],
                    op0=ALU.mult, op1=ALU.mult))
            for b in range(n // 128):
                yo = yo_r.next()
                for half in range(2):
                    p, off = ps1_tile(c)
                    for j in range(4):
                        cc = half * 4 + j
                        k.op("pe", [xn, c.ident_f], [p], lambda e, cc=cc, j=j, p=p, off=off, b=b: e.transpose(
                            p[:, off + j * 128:off + (j + 1) * 128], xn[:, cc, b * 128:(b + 1) * 128], c.ident_f[:]))
                    if half == 0:
                        k.op("act", [p], [yo], lambda e, p=p, off=off, yo=yo: e.copy(out=yo[:, 0:512], in_=p[:, off:off + 512]))
                    else:
                        k.op("dve", [p, yo], [yo], lambda e, p=p, off=off, yo=yo: e.tensor_copy(
                            out=yo[:, 512:1024], in_=p[:, off:off + 512]))
                k.dma(y_ap[t0 + b * 128:t0 + (b + 1) * 128, :], yo[:], [yo], [c.y], q="act")


def tiles_of(cfg, tmax):
    out = []
    for s0, T in zip(cfg.starts, cfg.seqs):
        a = 0
        while a < T:
            n = min(tmax, T - a)
            out.append((s0 + a, n))
            a += n
    return out


def phase_A(c, l):
    k = c.k
    cfg = c.cfg
    even = (l % 2 == 0)
    e = l // 2
    wname = "ev_w_in" if even else "od_w_in"
    W_ap = c.wb[wname].ap()
    Wd = c.wb[wname].part(e)
    row0 = e * D
    TA = 1024
    SC = 128.0 ** -0.5
    if even:
        fm_blocks = [(0, "copy", (0, 1.0)), (512, "hz", (0, 4)), (1024, "hz", (1, 8)), (2048, "silu", (12,))]
        for g in range(3):
            base = 2560 + g * 1536
            fm_blocks.append((base, "copy", (16 + g * 8, SC)))
            fm_blocks.append((base + 512, "copy", (20 + g * 8, 1.0)))
        tm_groups = [(1536, 0)] + [(2560 + g * 1536 + 1024, 1 + g) for g in range(3)]
    else:
        fm_blocks = [(0, "copy", (0, SC)), (512, "copy", (4, 1.0)), (2048, "silu", (8,)), (2560, "silu", (12,))]
        tm_groups = [(1024, 0), (1536, 1)]
    h_ap = c.h_d.ap().rearrange("c p t -> p c t")
    with Scope(c):
        hin_r = sb_ring(k, "a_hin", [128, DC, TA], F32, 1)
        sq_r = sb_ring(k, "a_sq", [128, DC, TA], BF16, 1)
        rstd_r = sb_ring(k, "a_rstd", [128, TA], F32, 2)
        xn_r = sb_ring(k, "a_xn", [128, DC, TA], BF16, 2)
        w_r = sb_ring(k, "a_w", [128, DC, 512], BF16, 3)
        stb_r = sb_ring(k, "a_stb", [128, TA], BF16, 4)
        stf_r = sb_ring(k, "a_stf", [128, TA], F32, 3)
        tmp_r = sb_ring(k, "a_tmp", [128, TA], F32, 3)
        sttm_r = sb_ring(k, "a_sttm", [128, TA // 128, 512], BF16, 2)
        lr_r = sb_ring(k, "a_lr", [16, TA], BF16, 2)
        wlr_r = sb_ring(k, "a_wlr", [128, DC, 16], BF16, 2)
        evac_i = [0]

        def load_w(col0, w):
            wt = w_r.next()
            k.dma(wt[:, :, :w], W_ap[row0:row0 + D, col0:col0 + w].rearrange("(c p) f -> p c f", p=128), [Wd], [wt])
            return wt

        def store_fm(dst, chunk, st, t0, n):
            k.dma(dst.ap()[chunk, :, t0:t0 + n], st[:, :n], [st], [dst.sel([chunk], t0, t0 + n)], q="act")

        for (t0, n) in tiles_of(cfg, TA):
            ht = hin_r.next()
            k.dma(ht[:, :, :n], h_ap[:, :, t0:t0 + n], [c.h_d.sel([0], t0, t0 + n)], [ht])
            xn = xn_r.next()
            norm_xn(c, ht, n, c.g_mix[:, l * DC:(l + 1) * DC], sq_r, rstd_r, xn)
            subs = subtiles(n)
            for (col0, kind, args) in fm_blocks:
                wt = load_w(col0, 512)
                for fc in range(4):
                    ps = c.ps2.next()
                    for (a, w) in subs:
                        for dc in range(DC):
                            k.op("pe", [wt, xn], [ps], lambda e_, fc=fc, dc=dc, a=a, w=w, ps=ps, wt=wt: e_.matmul(
                                ps[:, a:a + w], lhsT=wt[:, dc, fc * 128:(fc + 1) * 128], rhs=xn[:, dc, a:a + w],
                                start=(dc == 0), stop=(dc == DC - 1)))
                    if kind == "copy":
                        chunk0, scale = args
                        st = stb_r.next()
                        evac_i[0] += 1
                        if evac_i[0] % 2 == 0:
                            k.op("act", [ps], [st], lambda e_, ps=ps, st=st, scale=scale: e_.activation(
                                out=st[:, :n], in_=ps[:, :n], func=AF.Copy, scale=scale))
                        else:
                            k.op("dve", [ps], [st], lambda e_, ps=ps, st=st, scale=scale: e_.tensor_scalar(
                                out=st[:, :n], in0=ps[:, :n], scalar1=scale, scalar2=None, op0=ALU.mult))
                        store_fm(c.ufm, chunk0 + fc, st, t0, n)
                    elif kind == "silu":
                        chunk0, = args
                        st = stb_r.next()
                        k.op("act", [ps], [st], lambda e_, ps=ps, st=st: e_.activation(
                            out=st[:, :n], in_=ps[:, :n], func=AF.Silu))
                        store_fm(c.ufm, chunk0 + fc, st, t0, n)
                    elif kind == "hz":
                        d_, chunk0 = args
                        col = (d_ * 2 + e) * 4 + fc
                        sig = tmp_r.next()
                        k.op("act", [ps], [sig], lambda e_, ps=ps, sig=sig: e_.activation(
                            out=sig[:, :n], in_=ps[:, :n], func=AF.Sigmoid))
                        ff = tmp_r.next()
                        k.op("dve", [sig], [ff], lambda e_, sig=sig, ff=ff, col=col: e_.tensor_scalar(
                            out=ff[:, :n], in0=sig[:, :n], scalar1=c.oml[:, col:col + 1], scalar2=c.lb[:, col:col + 1],
                            op0=ALU.mult, op1=ALU.add))
                        gs = stf_r.next()
                        k.op("act", [ff], [gs], lambda e_, ff=ff, gs=gs: e_.activation(
                            out=gs[:, :n], in_=ff[:, :n], func=AF.Ln))
                        store_fm(c.gfm, d_ * 4 + fc, gs, t0, n)
                        st = stb_r.next()
                        k.op("dve", [sig], [st], lambda e_, sig=sig, st=st, col=col: e_.tensor_scalar(
                            out=st[:, :n], in0=sig[:, :n], scalar1=c.noml[:, col:col + 1], scalar2=c.oml[:, col:col + 1],
                            op0=ALU.mult, op1=ALU.add))
                        store_fm(c.ufm, chunk0 + fc, st, t0, n)
            if not even:
                for d_ in range(2):
                    wl = wlr_r.next()
                    cl0 = 3072 + 16 * d_
                    k.dma(wl[:], W_ap[row0:row0 + D, cl0:cl0 + 16].rearrange("(c p) f -> p c f", p=128), [Wd], [wl])
                    ps = c.ps2.next()
                    for (a, w) in subs:
                        for dc in range(DC):
                            k.op("pe", [wl, xn], [ps], lambda e_, dc=dc, a=a, w=w, ps=ps, wl=wl: e_.matmul(
                                ps[0:16, a:a + w], lhsT=wl[:, dc, :], rhs=xn[:, dc, a:a + w],
                                start=(dc == 0), stop=(dc == DC - 1)))
                    lr = lr_r.next()
                    k.op("act", [ps], [lr], lambda e_, ps=ps, lr=lr: e_.copy(out=lr[:, :n], in_=ps[0:16, :n]))
                    od = e * 2 + d_
                    for fc in range(4):
                        ps = c.ps2.next()
                        for (a, w) in subs:
                            k.op("pe", [lr, c.wgu], [ps], lambda e_, fc=fc, a=a, w=w, ps=ps, lr=lr, od=od: e_.matmul(
                                ps[:, a:a + w], lhsT=c.wgu[:, od, fc * 128:(fc + 1) * 128], rhs=lr[:, a:a + w],
                                start=True, stop=True))
                        col = od * 4 + fc
                        ex = tmp_r.next()
                        k.op("act", [ps], [ex], lambda e_, ps=ps, ex=ex, col=col: e_.activation(
                            out=ex[:, :n], in_=ps[:, :n], func=AF.Exp, scale=-1.0, bias=c.nb_gate[:, col:col + 1]))
                        ln = tmp_r.next()
                        k.op("act", [ex], [ln], lambda e_, ex=ex, ln=ln: e_.activation(
                            out=ln[:, :n], in_=ex[:, :n], func=AF.Ln, bias=1.0))
                        gs = stf_r.next()
                        k.op("dve", [ln], [gs], lambda e_, ln=ln, gs=gs: e_.tensor_scalar(
                            out=gs[:, :n], in0=ln[:, :n], scalar1=-1.0 / 16.0, scalar2=None, op0=ALU.mult))
                        store_fm(c.gfm, d_ * 4 + fc, gs, t0, n)
            for (col0, gidx) in tm_groups:
                wt = load_w(col0, 512)
                st = sttm_r.next()
                nb = n // 128
                for tb in range(nb):
                    p, off = ps1_tile(c)
                    for dc in range(DC):
                        k.op("pe", [wt, xn], [p], lambda e_, dc=dc, tb=tb, p=p, off=off, wt=wt: e_.matmul(
                            p[:, off:off + 512], lhsT=xn[:, dc, tb * 128:(tb + 1) * 128], rhs=wt[:, dc, :],
                            start=(dc == 0), stop=(dc == DC - 1)))
                    if tb % 2 == 0:
                        k.op("act", [p], [st], lambda e_, p=p, off=off, st=st, tb=tb: e_.copy(
                            out=st[:, tb, :], in_=p[:, off:off + 512]))
                    else:
                        k.op("dve", [p], [st], lambda e_, p=p, off=off, st=st, tb=tb: e_.tensor_copy(
                            out=st[:, tb, :], in_=p[:, off:off + 512]))
                k.dma(c.utm.ap()[gidx, t0:t0 + n, :].rearrange("(b p) f -> p b f", p=128), st[:, :nb, :],
                      [st], [c.utm.sel([gidx], t0, t0 + n)], q="act")


def phase_B(c, l):
    if l % 2 == 0:
        e = l // 2
        if "gla" in c.cfg.parts:
            gla(c, dv=128, q0=0, kf0=4, kb0=8, gate0=12, v_of=lambda h: (0, h * 128),
                gain=c.g_hg[:, e * 4:(e + 1) * 4], mix0=0)
        if "attn" in c.cfg.parts:
            attention(c)
    else:
        o = l // 2
        gla(c, dv=256, q0=0, kf0=4, kb0=4, gate0=8, v_of=lambda h: (h // 2, (h % 2) * 256),
            gain=c.g_gla[:, o * 8:(o + 1) * 8], mix0=0)


import os as _os0
_GSTOP = int(_os0.environ.get('GLA_STOP', '99'))


def gla(c, dv, q0, kf0, kb0, gate0, v_of, gain, mix0):
    k = c.k
    cfg = c.cfg
    dvc = dv // 128
    G = 512
    ufm, gfm, utm = c.ufm.ap(), c.gfm.ap(), c.utm.ap()
    with Scope(c):
        gla_constants(c)
        TMAX = max(cfg.seqs)
        oacc_r = sb_ring(k, "g_oacc", [128, dvc, TMAX], F32, 2)
        q_r = sb_ring(k, "g_q", [128, G], BF16, 5)
        kk_r = sb_ring(k, "g_k", [128, G], BF16, 5)
        g_r = sb_ring(k, "g_g", [128, G], F32, 5)
        v_r = sb_ring(k, "g_v", [64, 8, dv], BF16, 5)
        b_r = sb_ring(k, "g_b", [128, G], F32, 2)
        d_r = sb_ring(k, "g_d", [128, G], F32, 2)
        eq_r = sb_ring(k, "g_eq", [128, G], F32, 2)
        ek_r = sb_ring(k, "g_ek", [128, G], F32, 2)
        qt_r = sb_ring(k, "g_qt", [128, G], BF16, 5)
        kt_r = sb_ring(k, "g_kt", [128, G], BF16, 5)
        ktm_r = sb_ring(k, "g_ktm", [64, 8, 128], BF16, 5)
        am_r = sb_ring(k, "g_am", [64, 8, 64], BF16, 5)
        c1_r = sb_ring(k, "g_c1", [128, 8], F32, 5)
        Sf2 = [[k.sb("g_Sf%d_%d" % (i, j), [128, dv], F32) for j in range(2)] for i in range(2)]
        Scur = [0, 0]
        sb_r = [sb_ring(k, "g_Sb%d_" % i, [128, dv], BF16, 3) for i in range(2)]
        Sb = [None, None]
        br_r = sb_ring(k, "g_br", [128, G], F32, 2)
        eh_r = sb_ring(k, "g_eh", [128, G], F32, 2)
        qh_r = sb_ring(k, "g_qh", [128, G], BF16, 5)
        sq_r = sb_ring(k, "g_sq", [128, dvc, G], BF16, 2)
        rstd_r = sb_ring(k, "g_rstd", [128, G], F32, 2)
        gate_r = sb_ring(k, "g_gate", [128, dvc, G], BF16, 2)
        on_r = sb_ring(k, "g_on", [128, G], F32, 2)
        out_r = sb_ring(k, "g_out", [128, dvc, G], BF16, 2)

        def g_front(s0, h, d_, gi, lk=None):
            t0 = s0 + gi * G
            kc = (kf0 if d_ == 0 else kb0) + h
            q = q_r.next()
            k.dma(q[:], ufm[q0 + h, :, t0:t0 + G], [c.ufm.sel([q0 + h], t0, t0 + G)], [q])
            kk = kk_r.next()
            k.dma(kk[:], ufm[kc, :, t0:t0 + G], [c.ufm.sel([kc], t0, t0 + G)], [kk])
            g = g_r.next()
            gch = d_ * 4 + h
            k.dma(g[:], gfm[gch, :, t0:t0 + G], [c.gfm.sel([gch], t0, t0 + G)], [g])
            v = v_r.next()
            vg, vc0 = v_of(h)
            k.dma(v[:], utm[vg, t0:t0 + G, vc0:vc0 + dv].rearrange("(b p) f -> p b f", p=64),
                  [c.utm.sel([vg], t0, t0 + G)], [v])
            yield
            b = b_r.next()
            k.op("dve", [c.rmask, g], [b], lambda e: e.tensor_tensor_scan(
                out=b[:], data0=c.rmask[:], data1=g[:], initial=0.0, op0=ALU.mult, op1=ALU.add))
            b3 = b[:].rearrange("p (c j) -> p c j", j=64)
            tot_b = b3[:, :, 63:64].broadcast_to([128, 8, 64])
            c1 = c1_r.next()
            k.op("act", [b], [c1], lambda e: e.activation(out=c1[:].rearrange("p (c o) -> p c o", o=1),
                                                           in_=b3[:, :, 63:64], func=AF.Exp))
            yield
            d = d_r.next()
            d3 = d[:].rearrange("p (c j) -> p c j", j=64)
            if d_ == 0:
                k.op("dve", [b], [d], lambda e: e.tensor_tensor(out=d3, in0=b3, in1=tot_b, op=ALU.subtract))
                bq = b
            else:
                k.op("dve", [b, g], [d], lambda e: e.tensor_tensor(out=d[:], in0=g[:], in1=b[:], op=ALU.subtract))
                bq = br_r.next()
                k.op("dve", [b, d], [bq], lambda e: e.tensor_tensor(
                    out=bq[:].rearrange("p (c j) -> p c j", j=64), in0=d3, in1=tot_b, op=ALU.add))
            yield
            eq = eq_r.next()
            k.op("act", [d], [eq], lambda e: e.activation(out=eq[:], in_=d[:], func=AF.Exp))
            ek = ek_r.next()
            k.op("act", [d], [ek], lambda e: e.activation(out=ek[:], in_=d[:], func=AF.Exp, scale=-1.0))
            eh_ = eh_r.next()
            k.op("act", [bq], [eh_], lambda e: e.activation(out=eh_[:], in_=bq[:], func=AF.Exp))
            yield
            kt = kt_r.next()
            k.op("pool", [kk, ek], [kt], lambda e: e.tensor_tensor(out=kt[:], in0=kk[:], in1=ek[:], op=ALU.mult))
            yield
            qt = qt_r.next()
            k.op("pool", [q, eq], [qt], lambda e: e.tensor_tensor(out=qt[:], in0=q[:], in1=eq[:], op=ALU.mult))
            yield
            qh = qh_r.next()
            k.op("pool", [q, eh_], [qh], lambda e: e.tensor_tensor(out=qh[:], in0=q[:], in1=eh_[:], op=ALU.mult))
            for _ in range(4):
                yield
            yield
            pT = c.psT.next()
            for ch in range(8):
                k.op("pe", [kt, c.ident_b], [pT], lambda e, ch=ch: e.transpose(
                    pT[0:64, ch * 128:(ch + 1) * 128], kt[:, ch * 64:(ch + 1) * 64], c.ident_b[:]))
            yield
            yield
            ktm = ktm_r.next()
            k.op("act", [pT], [ktm], lambda e: e.copy(out=ktm[:], in_=pT[0:64, :].rearrange("p (a f) -> p a f", a=8)))
            yield
            pA, offA = ps1_tile(c)
            for ch in range(8):
                k.op("pe", [kt, qt], [pA], lambda e, ch=ch: e.matmul(
                    pA[0:64, offA + ch * 64:offA + (ch + 1) * 64], lhsT=kt[:, ch * 64:(ch + 1) * 64],
                    rhs=qt[:, ch * 64:(ch + 1) * 64], start=True, stop=True))
            yield
            am = am_r.next()
            mask = c.maskF if d_ == 0 else c.maskB
            k.op("dve", [pA, mask], [am], lambda e: e.tensor_tensor(
                out=am[:], in0=pA[0:64, offA:offA + 512].rearrange("p (a t) -> p a t", a=8),
                in1=mask[0:64, 0:64].rearrange("p (o t) -> p o t", o=1).broadcast_to([64, 8, 64]), op=ALU.mult))
            yield
            cross = lk is not None and ((d_ == 0 and gi * G == lk) or (d_ == 1 and (gi + 1) * G == lk))
            return dict(d_=d_, gi=gi, v=v, ktm=ktm, qh=qh, c1=c1, am=am, cross=cross)

        def g_front2(st):
            d_, v, am = st["d_"], st["v"], st["am"]
            pO, offO = c.ps2.next(), 0
            for ch in range(8):
                for eh in range(dvc):
                    k.op("pe", [v, am], [pO], lambda e, ch=ch, eh=eh: e.matmul(
                        pO[:, offO + eh * 512 + ch * 64:offO + eh * 512 + (ch + 1) * 64],
                        lhsT=v[:, ch, eh * 128:(eh + 1) * 128], rhs=am[:, ch, :], start=(ch == 0), stop=False,
                        skip_group_check=True))
            cper = 512 // dv
            order = list(range(8)) if d_ == 0 else list(range(7, -1, -1))
            st.update(pO=pO, offO=offO, cper=cper, order=order, ps_=None)

        def g_step(st, oi):
            d_, v, ktm, qh, c1, pO, offO, cper, order = (st[x] for x in
                                                        ("d_", "v", "ktm", "qh", "c1", "pO", "offO", "cper", "order"))
            S = Sf2[d_][Scur[d_]]
            Sn = Sf2[d_][1 - Scur[d_]]
            ch = order[oi]
            if oi == 0 and st["cross"]:
                k.op("dve", [S, c.link], [S], lambda e: e.tensor_scalar(
                    out=S[:], in0=S[:], scalar1=c.link[:, 0:1], scalar2=None, op0=ALU.mult))
                nsb0 = sb_r[d_].next()
                k.op("act", [S], [nsb0], lambda e, nsb0=nsb0: e.copy(out=nsb0[:], in_=S[:]))
                Sb[d_] = nsb0
            if oi % cper == 0:
                st["ps_"] = ps1_tile(c)
                ps_, pso = st["ps_"]
                for ch2 in order[oi:oi + cper]:
                    col2 = pso + (ch2 % cper) * dv
                    k.op("pe", [ktm, v], [ps_], lambda e, ch2=ch2, ps_=ps_, col2=col2: e.matmul(
                        ps_[:, col2:col2 + dv], lhsT=ktm[:, ch2, :], rhs=v[:, ch2, :], start=True, stop=True))
            ps_, pso = st["ps_"]
            col = pso + (ch % cper) * dv
            sbf = Sb[d_]
            for eh in range(dvc):
                k.op("pe", [sbf, qh], [pO], lambda e, ch=ch, eh=eh, sbf=sbf: e.matmul(
                    pO[:, offO + eh * 512 + ch * 64:offO + eh * 512 + (ch + 1) * 64],
                    lhsT=sbf[:, eh * 128:(eh + 1) * 128], rhs=qh[:, ch * 64:(ch + 1) * 64], start=False, stop=True,
                    skip_group_check=True))
            k.op("dve", [S, c1, ps_], [Sn], lambda e, ch=ch, ps_=ps_, col=col: e.scalar_tensor_tensor(
                out=Sn[:], in0=S[:], scalar=c1[:, ch:ch + 1], in1=ps_[:, col:col + dv], op0=ALU.mult, op1=ALU.add))
            nsb = sb_r[d_].next()
            k.op("act", [Sn], [nsb], lambda e, nsb=nsb: e.copy(out=nsb[:], in_=Sn[:]))
            Sb[d_] = nsb
            Scur[d_] = 1 - Scur[d_]

        def g_finish(st, oacc, first):
            pO, offO, gi = st["pO"], st["offO"], st["gi"]
            src = pO[:, offO:offO + dvc * 512].rearrange("p (a t) -> p a t", a=dvc)
            dst = oacc[:, :, gi * G:(gi + 1) * G]
            if first:
                k.op("act", [pO], [oacc], lambda e: e.copy(out=dst, in_=src))
            else:
                k.op("dve", [pO, oacc], [oacc], lambda e: e.tensor_tensor(out=dst, in0=dst, in1=src, op=ALU.add))

        def run_gen(gen, nsteps=None):
            try:
                n = 0
                while nsteps is None or n < nsteps:
                    next(gen)
                    n += 1
            except StopIteration as stop:
                return stop.value
            return None

        def finalize(s0, h, ng, oacc):
            for gi in range(ng):
                t0 = s0 + gi * G
                osl = Buf(oacc.t, "oview", oacc.leaves)
                sq = sq_r.next()
                k.op("act", [oacc], [sq], lambda e, sq=sq, gi=gi: e.activation(
                    out=sq[:], in_=oacc[:, :, gi * G:(gi + 1) * G], func=AF.Square))
                p, off = ps1_tile(c)
                for eh in range(dvc):
                    k.op("pe", [sq, c.ones_b], [p], lambda e, eh=eh, p=p, off=off, sq=sq: e.matmul(
                        p[:, off:off + G], lhsT=c.ones_b[:], rhs=sq[:, eh, :], start=(eh == 0), stop=(eh == dvc - 1)))
                rstd = rstd_r.next()
                k.op("act", [p], [rstd], lambda e, p=p, off=off, rstd=rstd: e.activation(
                    out=rstd[:], in_=p[:, off:off + G], func=AF.Ln, scale=1.0 / dv, bias=EPS))
                k.op("act", [rstd], [rstd], lambda e, rstd=rstd: e.activation(
                    out=rstd[:], in_=rstd[:], func=AF.Exp, scale=-0.5))
                gate = gate_r.next()
                outb = out_r.next()
                for eh in range(dvc):
                    ch = gate0 + h * dvc + eh
                    k.dma(gate[:, eh, :], ufm[ch, :, t0:t0 + G], [c.ufm.sel([ch], t0, t0 + G)], [gate])
                for eh in range(dvc):
                    on = on_r.next()
                    k.op("pool", [oacc, rstd], [on], lambda e, on=on, eh=eh, gi=gi, rstd=rstd: e.tensor_tensor(
                        out=on[:], in0=oacc[:, eh, gi * G:(gi + 1) * G], in1=rstd[:], op=ALU.mult))
                    gcol = h * dvc + eh
                    k.op("dve", [on, gate], [outb], lambda e, on=on, eh=eh, gate=gate, outb=outb, gcol=gcol:
                         e.scalar_tensor_tensor(out=outb[:, eh, :], in0=on[:], scalar=gain[:, gcol:gcol + 1],
                                                in1=gate[:, eh, :], op0=ALU.mult, op1=ALU.mult))
                    mch = mix0 + h * dvc + eh
                    k.dma(c.mix.ap()[mch, :, t0:t0 + G], outb[:, eh, :], [outb], [c.mix.sel([mch], t0, t0 + G)], q="act")

        pairs = []
        for si, (s0, T) in enumerate(zip(cfg.starts, cfg.seqs)):
            ng = T // G
            for h in range(4):
                for i in range(ng):
                    pairs.append(dict(s0=s0, h=h, i=i, ng=ng, lk=cfg.links.get(si)))

        def fronts_of(P):
            return [g_front(P["s0"], P["h"], d_, gi, P["lk"]) for d_, gi in ((0, P["i"]), (1, P["ng"] - 1 - P["i"]))]

        cur = [run_gen(g) for g in fronts_of(pairs[0])]
        oacc = None
        seen_g = set()
        for n, P in enumerate(pairs):
            if P["i"] == 0:
                oacc = oacc_r.next()
                seen_g = set()
                for d_ in range(2):
                    k.op("pool", [], [Sf2[d_][Scur[d_]]], lambda e, d_=d_: e.memset(Sf2[d_][Scur[d_]][:], 0.0))
                    Sb[d_] = sb_r[d_].next()
                    k.op("pool", [], [Sb[d_]], lambda e, d_=d_: e.memset(Sb[d_][:], 0.0))
            chains = cur
            for st in chains:
                g_front2(st)
            nxt_gens = fronts_of(pairs[n + 1]) if n + 1 < len(pairs) else []
            nxt = []
            for oi in range(8):
                for st in chains:
                    g_step(st, oi)
                if len(nxt) < len(nxt_gens):
                    r_ = run_gen(nxt_gens[len(nxt)], 4)
                    if r_ is not None:
                        nxt.append(r_)
            while len(nxt) < len(nxt_gens):
                nxt.append(run_gen(nxt_gens[len(nxt)]))
            for st in chains:
                g_finish(st, oacc, st["gi"] not in seen_g)
                seen_g.add(st["gi"])
            if P["i"] == P["ng"] - 1:
                finalize(P["s0"], P["h"], P["ng"], oacc)
            cur = nxt


def attention(c):
    k = c.k
    cfg = c.cfg
    ufm, utm = c.ufm.ap(), c.utm.ap()
    with Scope(c):
        attn_constants(c)
        TMAX = max(cfg.seqs)
        acc_r = sb_ring(k, "t_acc", [128, 2, TMAX], F32, 1)
        qn_r = sb_ring(k, "t_qn", [128, TMAX], BF16, 2)
        kn_r = sb_ring(k, "t_kn", [128, TMAX], BF16, 2)
        qd_r = sb_ring(k, "t_qd", [128, TMAX], BF16, 2)
        kd_r = sb_ring(k, "t_kd", [128, TMAX], BF16, 2)
        NBLK = TMAX // 128 + 16
        vb_r = Ring([k.sb("t_vb%d" % i, [128, NBLK, 128], BF16).with_parts(NBLK) for i in range(2)])
        pb_r = sb_ring(k, "t_pb", [128, 256], BF16, 6)
        rec_r = sb_ring(k, "t_rec", [128, TMAX], F32, 1)
        ob_r = sb_ring(k, "t_ob", [128, TMAX], BF16, 2)
        for vb in vb_r.bufs:
            k.op("pool", [], [vb], lambda e, vb=vb: e.memset(vb[:], 0.0))
        for si, (s0, T) in enumerate(zip(cfg.starts, cfg.seqs)):
            lk = cfg.links.get(si)
            for h in range(4):
                acc = acc_r.next()
                pending = []
                for g, dil in enumerate(B_GROUPS_DIL):
                    L = T // dil
                    nb = L // 128
                    assert L % 128 == 0
                    bidx = g * 4 + h
                    qc, kc = 16 + g * 8 + h, 20 + g * 8 + h
                    qn = qn_r.next()
                    k.dma(qn[:, :T], ufm[qc, :, s0:s0 + T], [c.ufm.sel([qc], s0, s0 + T)], [qn])
                    kn = kn_r.next()
                    k.dma(kn[:, :T], ufm[kc, :, s0:s0 + T], [c.ufm.sel([kc], s0, s0 + T)], [kn])
                    if dil == 1:
                        qd, kd = qn, kn
                    else:
                        qd = qd_r.next()
                        k.op("pool", [qn], [qd], lambda e, qd=qd, qn=qn, dil=dil, L=L: e.tensor_copy(
                            out=qd[:, :T].rearrange("p (r m) -> p r m", r=dil),
                            in_=qn[:, :T].rearrange("p (m r) -> p r m", r=dil)))
                        kd = kd_r.next()
                        k.op("pool", [kn], [kd], lambda e, kd=kd, kn=kn, dil=dil, L=L: e.tensor_copy(
                            out=kd[:, :T].rearrange("p (r m) -> p r m", r=dil),
                            in_=kn[:, :T].rearrange("p (m r) -> p r m", r=dil)))
                    vb = vb_r.next()
                    vsrc = utm[1 + g, s0:s0 + T, h * 128:(h + 1) * 128].rearrange("(m r) f -> r m f", r=dil)
                    vrd = [c.utm.sel([1 + g], s0, s0 + T)]
                    for r in range(dil):
                        b0 = r * (nb + 1)
                        k.dma(vb[0:64, b0, :], vsrc[r, 0:64, :], vrd, [vb.part(b0)])
                        k.dma(vb[0:64, b0 + nb, :], vsrc[r, L - 64:L, :], vrd, [vb.part(b0 + nb)])
                        if nb > 1:
                            k.dma(vb[:, b0 + 1:b0 + nb, :],
                                  vsrc[r, 64:64 + 128 * (nb - 1), :].rearrange("(i p) f -> p i f", p=128), vrd,
                                  [Buf(vb.t, "vbi", vb.leaves[b0 + 1:b0 + nb])])
                    accv = acc[:, :, :T].rearrange("p a (m r) -> p a r m", r=dil)
                    for r in range(dil):
                        base = r * L
                        b0 = r * (nb + 1)
                        prev = None
                        for i in range(nb + 1):
                            pS, offS = ps1_tile(c)
                            pb = pb_r.next()
                            if i == 0:
                                kr, c0, c1_ = 64, 128, 256
                                kcols = (base, base + 64)
                                qcols = (base, base + 128)
                                idl = c.ident_b[0:64, 0:64]
                                bh = c.bias0_hi[:, bidx, :]
                                bl = c.bias0_lo[:, bidx, :]
                            elif i == nb:
                                kr, c0, c1_ = 64, 0, 128
                                kcols = (base + L - 64, base + L)
                                qcols = (base + 128 * (nb - 1), base + 128 * nb)
                                idl = c.ident_b[0:64, 0:64]
                                bh = c.bias_hi[0:64, bidx, 0:128]
                                bl = c.bias_lo[0:64, bidx, 0:128]
                            else:
                                kr, c0, c1_ = 128, 0, 256
                                kcols = (base + 128 * i - 64, base + 128 * i + 64)
                                qcols = (base + 128 * (i - 1), base + 128 * (i + 1))
                                idl = c.ident_b[:, :]
                                bh = c.bias_hi[:, bidx, :]
                                bl = c.bias_lo[:, bidx, :]
                            out = pS[0:kr, offS + c0:offS + c1_]
                            k.op("pe", [kd, qd], [pS], lambda e, out=out, kcols=kcols, qcols=qcols, kd=kd, qd=qd: e.matmul(
                                out, lhsT=kd[:, kcols[0]:kcols[1]], rhs=qd[:, qcols[0]:qcols[1]], start=True, stop=False))
                            k.op("pe", [c.bias_hi, c.bias0_hi], [pS], lambda e, out=out, idl=idl, bh=bh: e.matmul(
                                out, lhsT=idl, rhs=bh, start=False, stop=False))
                            k.op("pe", [c.bias_lo, c.bias0_lo], [pS], lambda e, out=out, idl=idl, bl=bl: e.matmul(
                                out, lhsT=idl, rhs=bl, start=False, stop=True))
                            k.op("act", [pS], [pb], lambda e, out=out, pb=pb, kr=kr, c0=c0, c1_=c1_: e.activation(
                                out=pb[0:kr, c0:c1_], in_=out, func=AF.Exp))
                            if lk is not None and 128 * i == lk // dil:
                                assert 0 < i < nb
                                k.op("dve", [pb, c.link], [pb], lambda e, pb=pb: e.tensor_scalar(
                                    out=pb[0:64, 128:256], in0=pb[0:64, 128:256], scalar1=c.link[0:64, 0:1],
                                    scalar2=None, op0=ALU.mult))
                                k.op("dve", [pb, c.link], [pb], lambda e, pb=pb: e.tensor_scalar(
                                    out=pb[64:128, 0:128], in0=pb[64:128, 0:128], scalar1=c.link[64:128, 0:1],
                                    scalar2=None, op0=ALU.mult))
                            if prev is not None:
                                def pv_job(j=i - 1, ppb=prev[0], pkr=prev[1], pb=pb, kr=kr, b0=b0, r=r, i=i, g=g,
                                           vb=vb, accv=accv, acc=acc):
                                    pO, offO = ps1_tile(c)
                                    for (lhs, dst) in ((None, 0), ("ones", 128)):
                                        for (pbuf, krr, blk, cs, st_, sp_) in ((ppb, pkr, b0 + j, 128, True, False),
                                                                                (pb, kr, b0 + i, 0, False, True)):
                                            if lhs is None:
                                                lt = vb[0:krr, blk, :]
                                                rd = [vb.part(blk), pbuf]
                                            else:
                                                lt = c.ones_b[0:krr, :]
                                                rd = [pbuf]
                                            k.op("pe", rd, [pO], lambda e, lt=lt, pbuf=pbuf, krr=krr, cs=cs, dst=dst,
                                                 st_=st_, sp_=sp_, pO=pO, offO=offO: e.matmul(
                                                pO[:, offO + dst:offO + dst + 128], lhsT=lt, rhs=pbuf[0:krr, cs:cs + 128],
                                                start=st_, stop=sp_))
                                    src = pO[:, offO:offO + 256].rearrange("p (a t) -> p a t", a=2)
                                    dstv = accv[:, :, r, 128 * j:128 * (j + 1)]
                                    if g == 0:
                                        k.op("act", [pO], [acc], lambda e, src=src, dstv=dstv: e.copy(out=dstv, in_=src))
                                    else:
                                        k.op("dve", [pO, acc], [acc], lambda e, src=src, dstv=dstv: e.tensor_tensor(
                                            out=dstv, in0=dstv, in1=src, op=ALU.add))
                                pending.append(pv_job)
                            while len(pending) > 1:
                                pending.pop(0)()
                            prev = (pb, kr)
                while pending:
                    pending.pop(0)()
                rec = rec_r.next()
                k.op("dve", [acc], [rec], lambda e, rec=rec, acc=acc: e.reciprocal(out=rec[:, :T], in_=acc[:, 1, :T]))
                ob = ob_r.next()
                k.op("pool", [acc, rec], [ob], lambda e, rec=rec, acc=acc, ob=ob: e.tensor_tensor(
                    out=ob[:, :T], in0=acc[:, 0, :T], in1=rec[:, :T], op=ALU.mult))
                k.dma(c.mix.ap()[4 + h, :, s0:s0 + T], ob[:, :T], [ob], [c.mix.sel([4 + h], s0, s0 + T)], q="act")


def windows_of(cfg, wmax):
    out = []
    for s0, T in zip(cfg.starts, cfg.seqs):
        nwin = 1
        while True:
            per = -(-T // nwin)
            per = -(-per // 128) * 128
            if per + 2 <= wmax or (nwin == 1 and T <= wmax):
                break
            nwin += 1
        a = 0
        while a < T:
            b = min(T, a + per)
            wa, wb = max(0, a - 1), min(T, b + 1)
            out.append((s0, T, s0 + a, s0 + b, s0 + wa, s0 + wb))
            a = b
    return out


def phase_C(c, l):
    k = c.k
    cfg = c.cfg
    even = (l % 2 == 0)
    e = l // 2
    Wo = c.wb["ev_w_out" if even else "od_w_out"].part(e)
    Wup, Wdn, Wpg, Wpp = (c.wb["ffn_w_up"].part(l), c.wb["ffn_w_down"].part(l), c.wb["ple_w_gate"].part(l),
                          c.wb["ple_w_proj"].part(l))
    WM = 1024
    h_ap = c.h_d.ap().rearrange("c p t -> p c t")
    mix_ap = c.mix.ap().rearrange("c p t -> p c t")
    hn_ap = c.h_n.ap().rearrange("c p t -> p c t")
    p_ap = c.ext["p"].ap()
    cw = c.conv_w[:].rearrange("p (l j f) -> p l j f", l=DEPTH, j=3)
    cwn = c.conv_w_nl[:].rearrange("p (l j f) -> p l j f", l=DEPTH, j=3)
    bounds = [s0_ + cfg.links[si] for si, s0_ in enumerate(cfg.starts) if si in cfg.links]
    cb = c.conv_b[:].rearrange("p (l f) -> p l f", l=DEPTH)
    with Scope(c):
        h_r = sb_ring(k, "c_h", [128, DC, WM], F32, 1)
        actbuf = k.sb("c_act", [128, NFF, WM], BF16).with_parts(NFF)
        sq_v = Buf(actbuf.t, "c_sq", actbuf.leaves[0:DC])
        mx_v = Buf(actbuf.t[:, DC:2 * DC, :], "c_mx", actbuf.leaves[DC:2 * DC])
        sq_r = Ring([sq_v])
        mx_r = Ring([mx_v])
        rstd_r = sb_ring(k, "c_rstd", [128, WM], F32, 1)
        xn_r = sb_ring(k, "c_xn", [128, DC, WM], BF16, 1)
        w_r = sb_ring(k, "c_w", [128, DC, 512], BF16, 4)
        wd_r = sb_ring(k, "c_wd", [128, NFF, 128], BF16, 2)
        wp_r = sb_ring(k, "c_wp", [128, 2, 512], BF16, 2)
        u_r = sb_ring(k, "c_u", [128, WM], F32, 2)
        ca_r = sb_ring(k, "c_ca", [128, WM], F32, 2)
        cg_r = sb_ring(k, "c_cg", [128, WM], F32, 2)
        sg_r = sb_ring(k, "c_sg", [128, WM], F32, 2)
        pin_r = sb_ring(k, "c_pin", [128, PLE], F32, 3)
        pT_r = sb_ring(k, "c_pT", [128, 2, WM], BF16, 1)

        def load_w(Wb, r0, col0, w):
            wt = w_r.next()
            k.dma(wt[:, :, :w], Wb.ap()[r0:r0 + D, col0:col0 + w].rearrange("(c p) f -> p c f", p=128), [Wb], [wt])
            return wt

        for (s0, T, a, b, wa, wb) in windows_of(cfg, WM):
            nw = wb - wa
            no = b - a
            oo = a - wa
            mx = mx_r.next()
            k.dma(mx[:, :, :nw], mix_ap[:, :, wa:wb], [c.mix.sel(range(8), wa, wb)], [mx])
            wt_first = load_w(Wo, e * D, 0, 512)
            ht = h_r.next()
            k.dma(ht[:, :, :nw], h_ap[:, :, wa:wb], [c.h_d.sel([0], wa, wb)], [ht])
            subs = subtiles(nw)
            osubs = [(oo + a_, w_) for (a_, w_) in subtiles(no)]
            for ob in range(2):
                wt = wt_first if ob == 0 else load_w(Wo, e * D, ob * 512, 512)
                for fc in range(4):
                    oc = ob * 4 + fc
                    ps = c.ps2.next()
                    for (sa, sw) in subs:
                        for dc in range(DC):
                            k.op("pe", [wt, mx], [ps], lambda e_, fc=fc, dc=dc, sa=sa, sw=sw, ps=ps, wt=wt: e_.matmul(
                                ps[:, sa:sa + sw], lhsT=wt[:, dc, fc * 128:(fc + 1) * 128], rhs=mx[:, dc, sa:sa + sw],
                                start=(dc == 0), stop=(dc == DC - 1)))
                    k.op("dve", [ps, ht], [ht], lambda e_, oc=oc, ps=ps: e_.tensor_tensor(
                        out=ht[:, oc, :nw], in0=ht[:, oc, :nw], in1=ps[:, :nw], op=ALU.add))
            xn = xn_r.next()
            norm_xn(c, ht, nw, c.g_ffn[:, l * DC:(l + 1) * DC], sq_r, rstd_r, xn)
            actT = actbuf
            for j in range(NFF):
                if j % 4 == 0:
                    nj = min(4, NFF - j)
                    wa_t = load_w(Wup, l * D, j * 128, nj * 128)
                    wg_t = load_w(Wup, l * D, D_FF + j * 128, nj * 128)
                jj = j % 4
                res = []
                for (wt, fidx, ring) in ((wg_t, NFF + j, cg_r), (wa_t, j, ca_r)):
                    pp = c.ps2.next()
                    for (sa, sw) in subs:
                        for dc in range(DC):
                            k.op("pe", [wt, xn], [pp], lambda e_, jj=jj, dc=dc, sa=sa, sw=sw, pp=pp, wt=wt: e_.matmul(
                                pp[:, sa:sa + sw], lhsT=wt[:, dc, jj * 128:(jj + 1) * 128], rhs=xn[:, dc, sa:sa + sw],
                                start=(dc == 0), stop=(dc == DC - 1)))
                    us = u_r.next()
                    k.op("act", [pp], [us], lambda e_, pp=pp, us=us: e_.copy(out=us[:, :nw], in_=pp[:, :nw]))
                    cc = ring.next()
                    k.op("act", [us], [cc], lambda e_, us=us, cc=cc, fidx=fidx: e_.activation(
                        out=cc[:, :nw], in_=us[:, :nw], func=AF.Identity, scale=cw[:, l, 1, fidx:fidx + 1],
                        bias=cb[:, l, fidx:fidx + 1]))
                    k.op("dve", [us, cc], [cc], lambda e_, us=us, cc=cc, fidx=fidx: e_.scalar_tensor_tensor(
                        out=cc[:, 1:nw], in0=us[:, 0:nw - 1], scalar=cw[:, l, 0, fidx:fidx + 1], in1=cc[:, 1:nw],
                        op0=ALU.mult, op1=ALU.add))
                    k.op("dve", [us, cc], [cc], lambda e_, us=us, cc=cc, fidx=fidx: e_.scalar_tensor_tensor(
                        out=cc[:, 0:nw - 1], in0=us[:, 1:nw], scalar=cw[:, l, 2, fidx:fidx + 1], in1=cc[:, 0:nw - 1],
                        op0=ALU.mult, op1=ALU.add))
                    for B_ in bounds:
                        if wa <= B_ - 1 and B_ <= wb - 1:
                            lb = B_ - wa
                            k.op("dve", [us, cc], [cc], lambda e_, us=us, cc=cc, fidx=fidx, lb=lb: e_.scalar_tensor_tensor(
                                out=cc[:, lb:lb + 1], in0=us[:, lb - 1:lb], scalar=cwn[:, l, 0, fidx:fidx + 1],
                                in1=cc[:, lb:lb + 1], op0=ALU.mult, op1=ALU.add))
                            k.op("dve", [us, cc], [cc], lambda e_, us=us, cc=cc, fidx=fidx, lb=lb: e_.scalar_tensor_tensor(
                                out=cc[:, lb - 1:lb], in0=us[:, lb:lb + 1], scalar=cwn[:, l, 2, fidx:fidx + 1],
                                in1=cc[:, lb - 1:lb], op0=ALU.mult, op1=ALU.add))
                    if fidx >= NFF:
                        pend_gelu = cc
                    else:
                        k.op("act", [pend_gelu], [pend_gelu], lambda e_, cc=pend_gelu: e_.activation(
                            out=cc[:, :nw], in_=cc[:, :nw], func=AF.Gelu))
                    res.append(cc)
                cgt, cat = res
                k.op("pool", [cat, cgt], [actT.part(j)], lambda e_, cat=cat, cgt=cgt, j=j: e_.tensor_tensor(
                    out=actT[:, j, :nw], in0=cat[:, :nw], in1=cgt[:, :nw], op=ALU.mult))
            for oc in range(DC):
                wd = wd_r.next()
                k.dma(wd[:], Wdn.ap()[l * D_FF:(l + 1) * D_FF, oc * 128:(oc + 1) * 128].rearrange("(j p) f -> p j f", p=128),
                      [Wdn], [wd])
                ps = c.ps2.next()
                for (sa, sw) in osubs:
                    for j in range(NFF):
                        k.op("pe", [wd, actT], [ps], lambda e_, j=j, sa=sa, sw=sw, ps=ps, wd=wd: e_.matmul(
                            ps[:, sa - oo:sa - oo + sw], lhsT=wd[:, j, :], rhs=actT[:, j, sa:sa + sw],
                            start=(j == 0), stop=(j == NFF - 1)))
                k.op("dve", [ps, ht], [ht], lambda e_, oc=oc, ps=ps: e_.tensor_tensor(
                    out=ht[:, oc, oo:oo + no], in0=ht[:, oc, oo:oo + no], in1=ps[:, :no], op=ALU.add))
            hv = Buf(ht.t, "hview", ht.leaves)
            xn2 = xn_r.next()
            rstd = rms_rstd_view(c, ht, oo, no, sq_r, rstd_r)
            for cc_ in range(DC):
                k.op("dve", [ht, rstd], [xn2], lambda e_, cc_=cc_: e_.scalar_tensor_tensor(
                    out=xn2[:, cc_, :no], in0=ht[:, cc_, oo:oo + no], scalar=c.g_ple[:, l * DC + cc_:l * DC + cc_ + 1],
                    in1=rstd[:, :no], op0=ALU.mult, op1=ALU.mult))
            pT = pT_r.next()
            for b0 in range(0, no, 128):
                nbk = min(128, no - b0)
                pin = pin_r.next()
                k.dma(pin[:nbk, :], p_ap[l, a + b0:a + b0 + nbk, :], [c.ext["p"]], [pin])
                pp, off = ps1_tile(c)
                for jj in range(2):
                    k.op("pe", [pin, c.ident_f], [pp], lambda e_, jj=jj, pp=pp, off=off, pin=pin, nbk=nbk: e_.transpose(
                        pp[:, off + jj * 128:off + jj * 128 + nbk], pin[:nbk, jj * 128:(jj + 1) * 128], c.ident_f[:nbk, :nbk]))
                k.op("act", [pp], [pT], lambda e_, pp=pp, off=off, b0=b0, nbk=nbk: e_.copy(
                    out=pT[:, :, b0:b0 + nbk], in_=pp[:, off:off + 256].rearrange("p (j t) -> p j t", j=2)[:, :, :nbk]))
            osub0 = subtiles(no)
            for ob in range(2):
                wt = load_w(Wpg, l * D, ob * 512, 512)
                wp = wp_r.next()
                k.dma(wp[:], Wpp.ap()[l * PLE:(l + 1) * PLE, ob * 512:(ob + 1) * 512].rearrange("(c p) f -> p c f", p=128),
                      [Wpp], [wp])
                for fc in range(4):
                    oc = ob * 4 + fc
                    ps = c.ps2.next()
                    for (sa, sw) in osub0:
                        for dc in range(DC):
                            k.op("pe", [wt, xn2], [ps], lambda e_, fc=fc, dc=dc, sa=sa, sw=sw, ps=ps, wt=wt: e_.matmul(
                                ps[:, sa:sa + sw], lhsT=wt[:, dc, fc * 128:(fc + 1) * 128], rhs=xn2[:, dc, sa:sa + sw],
                                start=(dc == 0), stop=(dc == DC - 1)))
                    sg = sg_r.next()
                    k.op("act", [ps], [sg], lambda e_, ps=ps, sg=sg: e_.activation(out=sg[:, :no], in_=ps[:, :no], func=AF.Sigmoid))
                    ps_p = c.ps2.next()
                    for (sa, sw) in osub0:
                        for dc in range(2):
                            k.op("pe", [wp, pT], [ps_p], lambda e_, fc=fc, dc=dc, sa=sa, sw=sw, ps_p=ps_p, wp=wp: e_.matmul(
                                ps_p[:, sa:sa + sw], lhsT=wp[:, dc, fc * 128:(fc + 1) * 128], rhs=pT[:, dc, sa:sa + sw],
                                start=(dc == 0), stop=(dc == 1)))
                    k.op("dve", [ps_p, sg], [sg], lambda e_, ps_p=ps_p, sg=sg: e_.tensor_tensor(
                        out=sg[:, :no], in0=sg[:, :no], in1=ps_p[:, :no], op=ALU.mult))
                    k.op("pool", [sg, ht], [ht], lambda e_, oc=oc, sg=sg: e_.tensor_tensor(
                        out=ht[:, oc, oo:oo + no], in0=ht[:, oc, oo:oo + no], in1=sg[:, :no], op=ALU.add))
            k.dma(hn_ap[:, :, a:b], ht[:, :, oo:oo + no], [ht], [c.h_n.sel([0], a, b)], q="act")
        c.h_d, c.h_n = c.h_n, c.h_d


def rms_rstd_view(c, hT, o0, n, sq_r, rstd_r):
    k = c.k
    sq = sq_r.next()
    k.op("act", [hT], [sq], lambda e: e.activation(out=sq[:, :DC, :n], in_=hT[:, :, o0:o0 + n], func=AF.Square))
    rstd = rstd_r.next()
    for (a, w) in subtiles(n):
        p, off = ps1_tile(c)
        for cc in range(DC):
            k.op("pe", [sq, c.ones_b], [p], lambda e, cc=cc, p=p, off=off, a=a, w=w: e.matmul(
                p[:, off:off + w], lhsT=c.ones_b[:], rhs=sq[:, cc, a:a + w], start=(cc == 0), stop=(cc == DC - 1)))
        k.op("act", [p], [rstd], lambda e, p=p, off=off, a=a, w=w: e.activation(
            out=rstd[:, a:a + w], in_=p[:, off:off + w], func=AF.Ln, scale=1.0 / D, bias=EPS))
    k.op("act", [rstd], [rstd], lambda e: e.activation(out=rstd[:, :n], in_=rstd[:, :n], func=AF.Exp, scale=-0.5))
    return rstd


def debug_dump(c):
    pass


def add_debug_outputs(c):
    k = c.k
    NT = c.cfg.nt
    for name, src, shape, dt in (("dbg_h", c.h_d, [DC, 128, NT], F32), ("dbg_ufm", c.ufm, [40, 128, NT], BF16),
                                 ("dbg_gfm", c.gfm, [8, 128, NT], F32), ("dbg_utm", c.utm, [4, NT, 512], BF16),
                                 ("dbg_mix", c.mix, [8, 128, NT], BF16)):
        dst = k.dram(name, shape, dt, kind="ExternalOutput")
        for i in range(shape[0]):
            k.dma(dst.ap()[i], src.ap()[i], [src], [dst])
        c.dbg.append(dst)


N_CORES = 8
SEQS_FULL = [4096, 2048]
LINKS_FULL = {0: 2048}
_CACHE = {}


def _core_tokens(x_prompt, x_sample, core):
    if core < 4:
        return np.concatenate([x_prompt[core], x_sample[core]], axis=0)
    j = 4 + 3 * (core - 4)
    return np.concatenate([x_sample[j], x_sample[j + 1], x_sample[j + 2]], axis=0)


def kernel(**inputs):
    cfg = Cfg(SEQS_FULL, links=LINKS_FULL)
    if "nc" not in _CACHE:
        _CACHE["nc"] = build(cfg)
    nc = _CACHE["nc"]
    xp = np.asarray(inputs["x_prompt"], dtype=np.float32)
    xs = np.asarray(inputs["x_sample"], dtype=np.float32)
    pp = np.asarray(inputs["p_prompt"], dtype=np.float32)
    psm = np.asarray(inputs["p_sample"], dtype=np.float32)
    params = {name: np.ascontiguousarray(np.asarray(inputs[name], dtype=np.float32)) for name, _ in PARAM_SHAPES}
    in_maps = []
    for core in range(N_CORES):
        m = dict(params)
        m["x"] = np.ascontiguousarray(_core_tokens(xp, xs, core))
        m["p"] = np.ascontiguousarray(np.stack(
            [_core_tokens(pp[l], psm[l], core) for l in range(DEPTH)], axis=0))
        m["link"] = np.full((128, 1), 1.0 if core < 4 else 0.0, dtype=np.float32)
        in_maps.append(m)
    res = run_bass_kernel_spmd(nc, in_maps, core_ids=list(range(N_CORES)))
    y_prompt = np.empty_like(xp)
    y_sample = np.empty_like(xs)
    for core in range(N_CORES):
        y = res.results[core]["y"]
        if core < 4:
            y_prompt[core] = y[0:4096]
            y_sample[core] = y[4096:6144]
        else:
            j = 4 + 3 * (core - 4)
            for t in range(3):
                y_sample[j + t] = y[2048 * t:2048 * (t + 1)]
    return (y_prompt, y_sample)
```

```python
import contextlib
import numpy as np
import concourse.bass as bass
import concourse.mybir as mybir
from concourse.bass_utils import run_bass_kernel_spmd

F32 = mybir.dt.float32
BF16 = mybir.dt.bfloat16
AF = mybir.ActivationFunctionType
ALU = mybir.AluOpType
AX = mybir.AxisListType

D = 1024
DC = 8
DEPTH = 4
PLE = 256
EPS = 1e-6
D_FF = 2816
EV_IN = 7168
OD_IN = 3104


class Tr:
    __slots__ = ("w", "r")

    def __init__(self):
        self.w = {}
        self.r = {}


class Buf:
    __slots__ = ("t", "leaves", "name", "grid")

    def __init__(self, t, name="", leaves=None):
        self.t = t
        self.leaves = leaves if leaves is not None else [Tr()]
        self.name = name
        self.grid = None

    def __getitem__(self, idx):
        return self.t[idx]

    def ap(self):
        return self.t.ap()

    def make_grid(self, n0, n1, blk):
        self.grid = (n0, n1, blk, [[Tr() for _ in range(n1)] for _ in range(n0)])
        self.leaves = [l for row in self.grid[3] for l in row]
        return self

    def sel(self, rows, t0, t1):
        n0, n1, blk, g = self.grid
        b0, b1 = t0 // blk, (t1 - 1) // blk
        lv = []
        for r in rows:
            lv.extend(g[r][b0:b1 + 1])
        return Buf(self.t, self.name, lv)

    def part(self, i):
        return Buf(self.t, self.name, [self.leaves[i]])

    def with_parts(self, n):
        self.leaves = [Tr() for _ in range(n)]
        return self


class K:
    def __init__(self, nc, es, n_dma_sems=64):
        self.nc = nc
        self.es = es
        self.eng = {"pe": nc.tensor, "act": nc.scalar, "dve": nc.vector, "pool": nc.gpsimd, "sp": nc.sync}
        self.cnt = {}
        self.seen = {e: {} for e in self.eng}
        self.semh = {}
        for e in self.eng:
            s = es.enter_context(nc.semaphore("c_" + e))
            self.semh[e] = s
            self.cnt[e] = 0
        self.dsem = []
        self.dtot = []
        for i in range(n_dma_sems):
            self.dsem.append(es.enter_context(nc.semaphore("d%d" % i)))
            self.dtot.append(0)
            self.semh[("d", i)] = self.dsem[i]
        self.dnext = 0
        self.dnext_sw = 0
        self.n_hw = n_dma_sems - 16
        self.n_ins = 0

    def sb(self, name, shape, dt):
        self.uid = getattr(self, "uid", 0) + 1
        name = "%s_u%d" % (name, self.uid)
        return Buf(self.es.enter_context(self.nc.sbuf_tensor(name, list(shape), dt)), name)

    def ps(self, name, shape, dt=F32):
        return Buf(self.es.enter_context(self.nc.psum_tensor(name, list(shape), dt)), name)

    def dram(self, name, shape, dt, kind="Internal"):
        return Buf(self.nc.dram_tensor(name, list(shape), dt, kind=kind), name)

    def _need(self, E, reads, writes):
        need = {}
        for b in reads:
            for l in b.leaves:
                for k, v in l.w.items():
                    if need.get(k, 0) < v:
                        need[k] = v
        for b in writes:
            for l in b.leaves:
                for k, v in l.w.items():
                    if need.get(k, 0) < v:
                        need[k] = v
                for k, v in l.r.items():
                    if need.get(k, 0) < v:
                        need[k] = v
        seen = self.seen[E]
        eng = self.eng[E]
        for k, v in need.items():
            if k == "pe" and E == "pe":
                continue
            if seen.get(k, 0) < v:
                eng.wait_ge(self.semh[k], v)
                seen[k] = v

    def _mark(self, tok, reads, writes):
        k, v = tok
        for b in reads:
            for l in b.leaves:
                if l.r.get(k, 0) < v:
                    l.r[k] = v
        for b in writes:
            for l in b.leaves:
                l.w = {k: v}
                l.r = {}

    def op(self, E, reads, writes, fn):
        self._need(E, reads, writes)
        ins = fn(self.eng[E])
        self.cnt[E] += 1
        ins.then_inc(self.semh[E], 1)
        self._mark((E, self.cnt[E]), reads, writes)
        self.n_ins += 1
        return ins

    def dma(self, out_ap, in_ap, reads, writes, q="sp", **kw):
        if q == "pool":
            i = self.n_hw + self.dnext_sw
            self.dnext_sw = (self.dnext_sw + 1) % (len(self.dsem) - self.n_hw)
        else:
            i = self.dnext
            self.dnext = (self.dnext + 1) % self.n_hw
        key = ("d", i)
        self._need(q, reads, writes)
        if self.dtot[i] > 0 and self.seen[q].get(key, 0) < self.dtot[i]:
            self.eng[q].wait_ge(self.dsem[i], self.dtot[i])
            self.seen[q][key] = self.dtot[i]
        ins = self.eng[q].dma_start(out=out_ap, in_=in_ap, **kw)
        self.dtot[i] += 16
        ins.then_inc(self.dsem[i], 16)
        self._mark((key, self.dtot[i]), reads, writes)
        self.n_ins += 1
        return ins

    def finish(self, bufs, q="sp"):
        for b in bufs:
            for l in b.leaves:
                for k, v in l.w.items():
                    if self.seen[q].get(k, 0) < v:
                        self.eng[q].wait_ge(self.semh[k], v)
                        self.seen[q][k] = v


class Ring:
    def __init__(self, bufs):
        self.bufs = bufs
        self.i = 0

    def next(self):
        b = self.bufs[self.i]
        self.i = (self.i + 1) % len(self.bufs)
        return b


def sb_ring(k, name, shape, dt, n):
    return Ring([k.sb("%s%d" % (name, i), shape, dt) for i in range(n)])


class Cfg:
    def __init__(self, seqs, n_layers=DEPTH, stop_after=None, debug=False, links=None):
        self.seqs = list(seqs)
        self.nt = sum(seqs)
        self.n_layers = n_layers
        self.starts = [sum(seqs[:i]) for i in range(len(seqs))]
        self.stop_after = stop_after
        self.debug = debug
        self.parts = ("gla", "attn")
        self.links = dict(links or {})


PARAM_SHAPES = [
    ("norm_mix_g", [DEPTH, D]), ("ev_w_in", [2, D, EV_IN]), ("hgrn_lb_logits", [2, 2, 512]),
    ("hgrn_norm_g", [2, 512]), ("ev_w_out", [2, D, D]), ("od_w_in", [2, D, OD_IN]),
    ("gla_w_gate_up", [2, 2, 16, 512]), ("gla_b_gate", [2, 2, 512]), ("gla_norm_g", [2, D]),
    ("od_w_out", [2, D, D]), ("norm_ffn_g", [DEPTH, D]), ("ffn_w_up", [DEPTH, D, 2 * D_FF]),
    ("ffn_conv_w", [DEPTH, 3, 2 * D_FF]), ("ffn_conv_b", [DEPTH, 2 * D_FF]),
    ("ffn_w_down", [DEPTH, D_FF, D]), ("norm_ple_g", [DEPTH, D]), ("ple_w_gate", [DEPTH, D, D]),
    ("ple_w_proj", [DEPTH, PLE, D]), ("norm_out_g", [D]),
]
BIG_W = ["ev_w_in", "ev_w_out", "od_w_in", "od_w_out", "ffn_w_up", "ffn_w_down", "ple_w_gate", "ple_w_proj"]
B_GROUPS_DIL = (1, 4, 16)
NFF = D_FF // 128
NEG = -1.0e30


class Ctx:
    pass


def subtiles(n):
    out = []
    a = 0
    while a < n:
        w = min(512, n - a)
        out.append((a, w))
        a += w
    return out


def build(cfg):
    nc = bass.Bass("TRN2", target_bir_lowering=False)
    NT = cfg.nt
    es = contextlib.ExitStack()
    with es:
        k = K(nc, es)
        c = Ctx()
        c.k, c.cfg, c.nc = k, cfg, nc
        c.ext = {}
        c.ext["x"] = k.dram("x", [NT, D], F32, kind="ExternalInput")
        c.ext["p"] = k.dram("p", [DEPTH, NT, PLE], F32, kind="ExternalInput")
        c.ext["link"] = k.dram("link", [128, 1], F32, kind="ExternalInput")
        for name, shape in PARAM_SHAPES:
            c.ext[name] = k.dram(name, shape, F32, kind="ExternalInput")
        c.y = k.dram("y", [NT, D], F32, kind="ExternalOutput")
        nb128 = NT // 128
        c.h_d = k.dram("h_scr", [DC, 128, NT], F32).make_grid(1, nb128, 128)
        c.h_n = k.dram("h_scr2", [DC, 128, NT], F32).make_grid(1, nb128, 128)
        c.ufm = k.dram("ufm_scr", [40, 128, NT], BF16).make_grid(40, nb128, 128)
        c.gfm = k.dram("gfm_scr", [8, 128, NT], F32).make_grid(8, nb128, 128)
        c.utm = k.dram("utm_scr", [4, NT, 512], BF16).make_grid(4, nb128, 128)
        c.mix = k.dram("mix_scr", [8, 128, NT], BF16).make_grid(8, nb128, 128)
        c.wb = {}
        for name in BIG_W:
            shp = dict(PARAM_SHAPES)[name]
            c.wb[name] = k.dram("wb_" + name, [shp[0] * shp[1], shp[2]], BF16).with_parts(shp[0])

        c.ps2 = Ring([k.ps("ps2_%d" % i, [128, 1024]) for i in range(2)])
        c.ps1 = Ring([(k.ps("ps1_%d" % i, [128, 512]), 0) for i in range(3)])
        psT = k.ps("psT", [128, 1024], BF16).with_parts(2)
        c.psT = Ring([psT])

        setup_constants(c)
        if cfg.debug:
            with Scope(c):
                z = k.sb("dbg_zero", [128, NT], BF16)
                k.op("pool", [], [z], lambda e: e.memset(z[:], 0.0))
                for i in range(8):
                    k.dma(c.mix.ap()[i], z[:], [z], [c.mix])
        precompute_mixer_constants(c)
        cast_weights(c)
        phase0(c)
        done = False
        for l in range(cfg.n_layers):
            for ph in ("A", "B", "C"):
                if ph == "A":
                    phase_A(c, l)
                elif ph == "B":
                    phase_B(c, l)
                else:
                    phase_C(c, l)
                if cfg.stop_after == (ph, l):
                    done = True
                    break
            if done:
                break
        final_norm(c)
        c.dbg = []
        if cfg.debug:
            add_debug_outputs(c)
        k.finish([c.y] + c.dbg)
    c.n_ins = k.n_ins
    print('[kernel] instructions:', k.n_ins, 'per-engine:', dict(k.cnt))
    return nc


def ps1_tile(c):
    b, off = c.ps1.next()
    return b, off


def setup_constants(c):
    k = c.k
    c.ident_f = k.sb("ident_f", [128, 128], F32)
    c.ident_b = k.sb("ident_b", [128, 128], BF16)
    c.ones_b = k.sb("ones_b", [128, 128], BF16)
    k.op("pool", [], [c.ident_f], lambda e: e.memset(c.ident_f[:], 0.0))
    k.op("pool", [c.ident_f], [c.ident_f], lambda e: e.affine_select(
        out=c.ident_f[:], in_=c.ident_f[:], pattern=[[-1, 128]], compare_op=ALU.not_equal,
        fill=1.0, base=0, channel_multiplier=1))
    k.op("pool", [c.ident_f], [c.ident_b], lambda e: e.tensor_copy(out=c.ident_b[:], in_=c.ident_f[:]))
    k.op("pool", [], [c.ones_b], lambda e: e.memset(c.ones_b[:], 1.0))

    def load_small(name, pattern, ncol, **kw):
        t = k.sb("p_" + name, [128, ncol], F32)
        k.dma(t[:], c.ext[name].ap().rearrange(pattern, p=128, **kw), [c.ext[name]], [t],
              allow_slow_non_contiguous=True)
        return t

    c.g_mix = load_small("norm_mix_g", "l (c p) -> p (l c)", DEPTH * DC)
    c.g_ffn = load_small("norm_ffn_g", "l (c p) -> p (l c)", DEPTH * DC)
    c.g_ple = load_small("norm_ple_g", "l (c p) -> p (l c)", DEPTH * DC)
    c.g_out = load_small("norm_out_g", "(c p) -> p c", DC)
    c.lbl = load_small("hgrn_lb_logits", "d e (c p) -> p (d e c)", 16)
    c.g_hg = load_small("hgrn_norm_g", "e (c p) -> p (e c)", 8)
    c.b_gate = load_small("gla_b_gate", "o d (c p) -> p (o d c)", 16)
    c.g_gla = load_small("gla_norm_g", "o (c p) -> p (o c)", 16)
    c.conv_w = load_small("ffn_conv_w", "l j (c p) -> p (l j c)", DEPTH * 3 * 2 * NFF)
    c.conv_b = load_small("ffn_conv_b", "l (c p) -> p (l c)", DEPTH * 2 * NFF)
    c.link = k.sb("link_sb", [128, 1], F32)
    k.dma(c.link[:], c.ext["link"].ap(), [c.ext["link"]], [c.link])
    c.conv_w_nl = k.sb("conv_w_nl", [128, DEPTH * 3 * 2 * NFF], F32)
    lm1 = k.sb("link_m1", [128, 1], F32)
    k.op("dve", [c.link], [lm1], lambda e: e.tensor_scalar(
        out=lm1[:], in0=c.link[:], scalar1=-1.0, scalar2=None, op0=ALU.add))
    k.op("dve", [c.conv_w, lm1], [c.conv_w_nl], lambda e: e.tensor_scalar(
        out=c.conv_w_nl[:], in0=c.conv_w[:], scalar1=lm1[:, 0:1], scalar2=None, op0=ALU.mult))
    c.nb_gate = k.sb("nb_gate", [128, 16], F32)
    k.op("dve", [c.b_gate], [c.nb_gate], lambda e: e.tensor_scalar(
        out=c.nb_gate[:], in0=c.b_gate[:], scalar1=-1.0, scalar2=None, op0=ALU.mult))
    c.lb = k.sb("lb", [128, 16], F32)
    c.oml = k.sb("oml", [128, 16], F32)
    c.noml = k.sb("noml", [128, 16], F32)
    k.op("pool", [], [c.lb], lambda e: e.memset(c.lb[:], 0.0))
    lb4 = c.lb[:].rearrange("p (d e c) -> p d e c", d=2, e=2)
    ll4 = c.lbl[:].rearrange("p (d e c) -> p d e c", d=2, e=2)
    k.op("dve", [c.lbl, c.lb], [c.lb], lambda e: e.tensor_tensor(
        out=lb4[:, :, 1, :], in0=ll4[:, :, 1, :], in1=ll4[:, :, 0, :], op=ALU.subtract))
    k.op("act", [c.lb], [c.lb], lambda e: e.activation(out=lb4[:, :, 1, :], in_=lb4[:, :, 1, :], func=AF.Sigmoid))
    k.op("dve", [c.lb], [c.oml], lambda e: e.tensor_scalar(
        out=c.oml[:], in0=c.lb[:], scalar1=-1.0, scalar2=1.0, op0=ALU.mult, op1=ALU.add))
    k.op("dve", [c.oml], [c.noml], lambda e: e.tensor_scalar(
        out=c.noml[:], in0=c.oml[:], scalar1=-1.0, scalar2=None, op0=ALU.mult))
    wgu_f = k.sb("wgu_f", [16, 4, 512], F32)
    k.dma(wgu_f[:], c.ext["gla_w_gate_up"].ap().rearrange("o d r c -> r (o d) c"), [c.ext["gla_w_gate_up"]], [wgu_f])
    c.wgu = k.sb("wgu_b", [16, 4, 512], BF16)
    k.op("dve", [wgu_f], [c.wgu], lambda e: e.tensor_copy(out=c.wgu[:], in_=wgu_f[:]))


def _gla_constants_compute(c):
    k = c.k
    c.rmask = k.sb("rmask", [128, 512], F32)
    k.op("pool", [], [c.rmask], lambda e: e.memset(c.rmask[:], 1.0))
    k.op("pool", [c.rmask], [c.rmask], lambda e: e.memset(
        c.rmask[:].rearrange("p (c j) -> p c j", j=64)[:, :, 0:1], 0.0))
    c.maskF = k.sb("maskF", [128, 128], F32)
    c.maskB = k.sb("maskB", [128, 128], F32)
    for m, sgn, (r0, c0) in ((c.maskF, 1, (0, 64)), (c.maskB, -1, (64, 0))):
        k.op("pool", [], [m], lambda e, m=m: e.memset(m[:], 1.0))
        k.op("pool", [m], [m], lambda e, m=m, sgn=sgn: e.affine_select(
            out=m[:], in_=m[:], pattern=[[sgn, 128]], compare_op=ALU.is_ge, fill=0.0, base=0,
            channel_multiplier=-sgn))
        k.op("pool", [m], [m], lambda e, m=m, r0=r0, c0=c0: e.memset(m[r0:r0 + 64, c0:c0 + 64], 0.0))


def _attn_constants_compute(c):
    k = c.k
    c.bias_hi = k.sb("bias_hi", [128, 12, 256], BF16)
    c.bias_lo = k.sb("bias_lo", [128, 12, 256], BF16)
    rel = k.sb("rel_f", [128, 256], F32)
    tmpb = k.sb("tmp_bias", [128, 256], F32)
    tmph = k.sb("tmp_biash", [128, 256], F32)
    k.op("pool", [], [rel], lambda e: e.iota(rel[:], pattern=[[-1, 256]], base=64, channel_multiplier=1,
                                              allow_small_or_imprecise_dtypes=True))
    k.op("act", [rel], [rel], lambda e: e.activation(out=rel[:], in_=rel[:], func=AF.Abs))
    for g in range(3):
        for h in range(4):
            idx = g * 4 + h
            slope = 2.0 ** (-8.0 * (idx + 1) / 12.0)
            coef = -slope * B_GROUPS_DIL[g]
            k.op("dve", [rel], [tmpb], lambda e, coef=coef: e.tensor_scalar(
                out=tmpb[:], in0=rel[:], scalar1=coef, scalar2=None, op0=ALU.mult))
            k.op("pool", [tmpb], [tmpb], lambda e: e.affine_select(
                out=tmpb[:], in_=tmpb[:], pattern=[[1, 256]], compare_op=ALU.is_ge, fill=NEG, base=0,
                channel_multiplier=-1))
            k.op("pool", [tmpb], [tmpb], lambda e: e.affine_select(
                out=tmpb[:], in_=tmpb[:], pattern=[[-1, 256]], compare_op=ALU.is_ge, fill=NEG, base=128,
                channel_multiplier=1))
            k.op("dve", [tmpb], [c.bias_hi], lambda e, idx=idx: e.tensor_copy(out=c.bias_hi[:, idx, :], in_=tmpb[:]))
            k.op("dve", [tmpb, c.bias_hi], [tmph], lambda e, idx=idx: e.tensor_tensor(
                out=tmph[:], in0=tmpb[:], in1=c.bias_hi[:, idx, :], op=ALU.subtract))
            k.op("dve", [tmph], [c.bias_lo], lambda e, idx=idx: e.tensor_copy(out=c.bias_lo[:, idx, :], in_=tmph[:]))
    c.bias0_hi = k.sb("bias0_hi", [64, 12, 128], BF16)
    c.bias0_lo = k.sb("bias0_lo", [64, 12, 128], BF16)
    k.dma(c.bias0_hi[:], c.bias_hi[64:128, :, 128:256], [c.bias_hi], [c.bias0_hi])
    k.dma(c.bias0_lo[:], c.bias_lo[64:128, :, 128:256], [c.bias_lo], [c.bias0_lo])


_GLA_CONSTS = (("rmask", [128, 512], F32), ("maskF", [128, 128], F32), ("maskB", [128, 128], F32))
_ATTN_CONSTS = (("bias_hi", [128, 12, 256], BF16), ("bias_lo", [128, 12, 256], BF16),
                ("bias0_hi", [64, 12, 128], BF16), ("bias0_lo", [64, 12, 128], BF16))


def precompute_mixer_constants(c):
    k = c.k
    c.cst = {}
    with Scope(c):
        _gla_constants_compute(c)
        _attn_constants_compute(c)
        for name, shape, dt in _GLA_CONSTS + _ATTN_CONSTS:
            d = k.dram("cst_" + name, shape, dt)
            k.dma(d.ap(), getattr(c, name)[:], [getattr(c, name)], [d])
            c.cst[name] = d


def _load_consts(c, specs):
    k = c.k
    for name, shape, dt in specs:
        t = k.sb(name, shape, dt)
        k.dma(t[:], c.cst[name].ap(), [c.cst[name]], [t])
        setattr(c, name, t)


def gla_constants(c):
    _load_consts(c, _GLA_CONSTS)


def attn_constants(c):
    _load_consts(c, _ATTN_CONSTS)


def barrier(c):
    k = c.k
    for E in k.eng:
        for E2 in k.eng:
            if E2 != E and k.cnt[E2] > 0 and k.seen[E].get(E2, 0) < k.cnt[E2]:
                k.eng[E].wait_ge(k.semh[E2], k.cnt[E2])
                k.seen[E][E2] = k.cnt[E2]
        for i, tot in enumerate(k.dtot[:k.n_hw]):
            key = ("d", i)
            if tot > 0 and k.seen[E].get(key, 0) < tot:
                k.eng[E].wait_ge(k.dsem[i], tot)
                k.seen[E][key] = tot


class Scope:
    def __init__(self, c):
        self.c = c

    def __enter__(self):
        self.saved = self.c.k.es
        self.es = contextlib.ExitStack()
        self.es.__enter__()
        self.c.k.es = self.es
        return self

    def __exit__(self, *a):
        barrier(self.c)
        self.c.k.es = self.saved
        return self.es.__exit__(*a)


def cast_weights(c):
    k = c.k
    order = []
    for l in range(DEPTH):
        e = l // 2
        order += [("ev_w_in", e), ("ev_w_out", e)] if l % 2 == 0 else [("od_w_in", e), ("od_w_out", e)]
        order += [("ffn_w_up", l), ("ffn_w_down", l), ("ple_w_gate", l), ("ple_w_proj", l)]
    for name, li in order:
        shp = dict(PARAM_SHAPES)[name]
        rows = shp[1]
        src = c.ext[name].ap()
        dst = c.wb[name].ap()
        RB = 256
        for r0 in range(0, rows, RB):
            r1 = min(rows, r0 + RB)
            k.dma(dst[li * rows + r0:li * rows + r1, :], src[li, r0:r1, :], [c.ext[name]], [c.wb[name].part(li)],
                  q="pool")


def phase0(c):
    k = c.k
    NT = c.cfg.nt
    with Scope(c):
        xin_r = sb_ring(k, "xin", [128, D], F32, 3)
        hT_r = sb_ring(k, "hT", [128, DC, 128], F32, 3)
        x_ap = c.ext["x"].ap()
        h_ap = c.h_d.ap().rearrange("c p t -> p c t")
        for b in range(NT // 128):
            xt = xin_r.next()
            k.dma(xt[:], x_ap[b * 128:(b + 1) * 128, :], [c.ext["x"]], [xt])
            ht = hT_r.next()
            for half in range(2):
                p, off = ps1_tile(c)
                for j in range(4):
                    cc = half * 4 + j
                    k.op("pe", [xt, c.ident_f], [p], lambda e, cc=cc, j=j, p=p, off=off, xt=xt: e.transpose(
                        p[:, off + j * 128:off + (j + 1) * 128], xt[:, cc * 128:(cc + 1) * 128], c.ident_f[:]))
                src = p[:, off:off + 512].rearrange("p (c t) -> p c t", c=4)
                if half == 0:
                    k.op("act", [p], [ht], lambda e, src=src, ht=ht: e.copy(out=ht[:, 0:4, :], in_=src))
                else:
                    k.op("dve", [p, ht], [ht], lambda e, src=src, ht=ht: e.tensor_copy(out=ht[:, 4:8, :], in_=src))
            k.dma(h_ap[:, :, b * 128:(b + 1) * 128], ht[:], [ht], [c.h_d.sel([0], b * 128, (b + 1) * 128)], q="act")


def rms_rstd(c, hT, n, sq_r, rstd_r, nchunks=DC, dim=D):
    k = c.k
    sq = sq_r.next()
    k.op("act", [hT], [sq], lambda e: e.activation(out=sq[:, :nchunks, :n], in_=hT[:, :nchunks, :n], func=AF.Square))
    rstd = rstd_r.next()
    for (a, w) in subtiles(n):
        p, off = ps1_tile(c)
        for cc in range(nchunks):
            k.op("pe", [sq, c.ones_b], [p], lambda e, cc=cc, p=p, off=off, a=a, w=w: e.matmul(
                p[:, off:off + w], lhsT=c.ones_b[:], rhs=sq[:, cc, a:a + w], start=(cc == 0), stop=(cc == nchunks - 1)))
        k.op("act", [p], [rstd], lambda e, p=p, off=off, a=a, w=w: e.activation(
            out=rstd[:, a:a + w], in_=p[:, off:off + w], func=AF.Ln, scale=1.0 / dim, bias=EPS))
    k.op("act", [rstd], [rstd], lambda e: e.activation(out=rstd[:, :n], in_=rstd[:, :n], func=AF.Exp, scale=-0.5))
    return rstd


def norm_xn(c, hT, n, gcols, sq_r, rstd_r, xn):
    k = c.k
    rstd = rms_rstd(c, hT, n, sq_r, rstd_r)
    for cc in range(DC):
        k.op("dve", [hT, rstd], [xn], lambda e, cc=cc: e.scalar_tensor_tensor(
            out=xn[:, cc, :n], in0=hT[:, cc, :n], scalar=gcols[:, cc:cc + 1], in1=rstd[:, :n],
            op0=ALU.mult, op1=ALU.mult))
    return xn


def final_norm(c):
    k = c.k
    NT = c.cfg.nt
    TT = 512
    with Scope(c):
        hin_r = sb_ring(k, "f_hin", [128, DC, TT], F32, 2)
        sq_r = sb_ring(k, "f_sq", [128, DC, TT], BF16, 2)
        rstd_r = sb_ring(k, "f_rstd", [128, TT], F32, 2)
        xn_r = sb_ring(k, "f_xn", [128, DC, TT], F32, 2)
        yo_r = sb_ring(k, "f_yo", [128, D], F32, 3)
        h_ap = c.h_d.ap().rearrange("c p t -> p c t")
        y_ap = c.y.ap()
        for t0 in range(0, NT, TT):
            n = min(TT, NT - t0)
            ht = hin_r.next()
            k.dma(ht[:, :, :n], h_ap[:, :, t0:t0 + n], [c.h_d.sel([0], t0, t0 + n)], [ht])
            rstd = rms_rstd(c, ht, n, sq_r, rstd_r)
            xn = xn_r.next()
            for cc in range(DC):
                k.op("dve", [ht, rstd], [xn], lambda e, cc=cc: e.scalar_tensor_tensor(
                    out=xn[:, cc, :n], in0=ht[:, cc, :n], scalar=c.g_out[:, cc:cc + 1], in1=rstd[:, :n],
                    op0=ALU.mult, op1=ALU.mult))
            for b in range(n // 128):
                yo = yo_r.next()
                for half in range(2):
                    p, off = ps1_tile(c)
                    for j in range(4):
                        cc = half * 4 + j
                        k.op("pe", [xn, c.ident_f], [p], lambda e, cc=cc, j=j, p=p, off=off, b=b: e.transpose(
                            p[:, off + j * 128:off + (j + 1) * 128], xn[:, cc, b * 128:(b + 1) * 128], c.ident_f[:]))
                    if half == 0:
                        k.op("act", [p], [yo], lambda e, p=p, off=off, yo=yo: e.copy(out=yo[:, 0:512], in_=p[:, off:off + 512]))
                    else:
                        k.op("dve", [p, yo], [yo], lambda e, p=p, off=off, yo=yo: e.tensor_copy(
                            out=yo[:, 512:1024], in_=p[:, off:off + 512]))
                k.dma(y_ap[t0 + b * 128:t0 + (b + 1) * 128, :], yo[:], [yo], [c.y], q="act")


def tiles_of(cfg, tmax):
    out = []
    for s0, T in zip(cfg.starts, cfg.seqs):
        a = 0
        while a < T:
            n = min(tmax, T - a)
            out.append((s0 + a, n))
            a += n
    return out


def phase_A(c, l):
    k = c.k
    cfg = c.cfg
    even = (l % 2 == 0)
    e = l // 2
    wname = "ev_w_in" if even else "od_w_in"
    W_ap = c.wb[wname].ap()
    Wd = c.wb[wname].part(e)
    row0 = e * D
    TA = 1024
    SC = 128.0 ** -0.5
    if even:
        fm_blocks = [(0, "copy", (0, 1.0)), (512, "hz", (0, 4)), (1024, "hz", (1, 8)), (2048, "silu", (12,))]
        for g in range(3):
            base = 2560 + g * 1536
            fm_blocks.append((base, "copy", (16 + g * 8, SC)))
            fm_blocks.append((base + 512, "copy", (20 + g * 8, 1.0)))
        tm_groups = [(1536, 0)] + [(2560 + g * 1536 + 1024, 1 + g) for g in range(3)]
    else:
        fm_blocks = [(0, "copy", (0, SC)), (512, "copy", (4, 1.0)), (2048, "silu", (8,)), (2560, "silu", (12,))]
        tm_groups = [(1024, 0), (1536, 1)]
    h_ap = c.h_d.ap().rearrange("c p t -> p c t")
    with Scope(c):
        hin_r = sb_ring(k, "a_hin", [128, DC, TA], F32, 1)
        sq_r = sb_ring(k, "a_sq", [128, DC, TA], BF16, 1)
        rstd_r = sb_ring(k, "a_rstd", [128, TA], F32, 2)
        xn_r = sb_ring(k, "a_xn", [128, DC, TA], BF16, 2)
        w_r = sb_ring(k, "a_w", [128, DC, 512], BF16, 3)
        stb_r = sb_ring(k, "a_stb", [128, TA], BF16, 4)
        stf_r = sb_ring(k, "a_stf", [128, TA], F32, 3)
        tmp_r = sb_ring(k, "a_tmp", [128, TA], F32, 3)
        sttm_r = sb_ring(k, "a_sttm", [128, TA // 128, 512], BF16, 2)
        lr_r = sb_ring(k, "a_lr", [16, TA], BF16, 2)
        wlr_r = sb_ring(k, "a_wlr", [128, DC, 16], BF16, 2)
        evac_i = [0]

        def load_w(col0, w):
            wt = w_r.next()
            k.dma(wt[:, :, :w], W_ap[row0:row0 + D, col0:col0 + w].rearrange("(c p) f -> p c f", p=128), [Wd], [wt])
            return wt

        def store_fm(dst, chunk, st, t0, n):
            k.dma(dst.ap()[chunk, :, t0:t0 + n], st[:, :n], [st], [dst.sel([chunk], t0, t0 + n)], q="act")

        tiles_a = tiles_of(cfg, TA)

        def prep(ti):
            t0_, n_ = tiles_a[ti]
            ht_ = hin_r.next()
            k.dma(ht_[:, :, :n_], h_ap[:, :, t0_:t0_ + n_], [c.h_d.sel([0], t0_, t0_ + n_)], [ht_])
            xn_ = xn_r.next()
            norm_xn(c, ht_, n_, c.g_mix[:, l * DC:(l + 1) * DC], sq_r, rstd_r, xn_)
            return xn_

        xn_next = prep(0)
        for ti, (t0, n) in enumerate(tiles_a):
            xn = xn_next
            subs = subtiles(n)
            for (col0, kind, args) in fm_blocks:
                wt = load_w(col0, 512)
                for fc in range(4):
                    ps = c.ps2.next()
                    for (a, w) in subs:
                        for dc in range(DC):
                            k.op("pe", [wt, xn], [ps], lambda e_, fc=fc, dc=dc, a=a, w=w, ps=ps, wt=wt: e_.matmul(
                                ps[:, a:a + w], lhsT=wt[:, dc, fc * 128:(fc + 1) * 128], rhs=xn[:, dc, a:a + w],
                                start=(dc == 0), stop=(dc == DC - 1)))
                    if kind == "copy":
                        chunk0, scale = args
                        st = stb_r.next()
                        evac_i[0] += 1
                        if evac_i[0] % 2 == 0:
                            k.op("act", [ps], [st], lambda e_, ps=ps, st=st, scale=scale: e_.activation(
                                out=st[:, :n], in_=ps[:, :n], func=AF.Copy, scale=scale))
                        else:
                            k.op("dve", [ps], [st], lambda e_, ps=ps, st=st, scale=scale: e_.tensor_scalar(
                                out=st[:, :n], in0=ps[:, :n], scalar1=scale, scalar2=None, op0=ALU.mult))
                        store_fm(c.ufm, chunk0 + fc, st, t0, n)
                    elif kind == "silu":
                        chunk0, = args
                        st = stb_r.next()
                        k.op("act", [ps], [st], lambda e_, ps=ps, st=st: e_.activation(
                            out=st[:, :n], in_=ps[:, :n], func=AF.Silu))
                        store_fm(c.ufm, chunk0 + fc, st, t0, n)
                    elif kind == "hz":
                        d_, chunk0 = args
                        col = (d_ * 2 + e) * 4 + fc
                        sig = tmp_r.next()
                        k.op("act", [ps], [sig], lambda e_, ps=ps, sig=sig: e_.activation(
                            out=sig[:, :n], in_=ps[:, :n], func=AF.Sigmoid))
                        ff = tmp_r.next()
                        k.op("dve", [sig], [ff], lambda e_, sig=sig, ff=ff, col=col: e_.tensor_scalar(
                            out=ff[:, :n], in0=sig[:, :n], scalar1=c.oml[:, col:col + 1], scalar2=c.lb[:, col:col + 1],
                            op0=ALU.mult, op1=ALU.add))
                        gs = stf_r.next()
                        k.op("act", [ff], [gs], lambda e_, ff=ff, gs=gs: e_.activation(
                            out=gs[:, :n], in_=ff[:, :n], func=AF.Ln))
                        store_fm(c.gfm, d_ * 4 + fc, gs, t0, n)
                        st = stb_r.next()
                        k.op("dve", [sig], [st], lambda e_, sig=sig, st=st, col=col: e_.tensor_scalar(
                            out=st[:, :n], in0=sig[:, :n], scalar1=c.noml[:, col:col + 1], scalar2=c.oml[:, col:col + 1],
                            op0=ALU.mult, op1=ALU.add))
                        store_fm(c.ufm, chunk0 + fc, st, t0, n)
            if not even:
                for d_ in range(2):
                    wl = wlr_r.next()
                    cl0 = 3072 + 16 * d_
                    k.dma(wl[:], W_ap[row0:row0 + D, cl0:cl0 + 16].rearrange("(c p) f -> p c f", p=128), [Wd], [wl])
                    ps = c.ps2.next()
                    for (a, w) in subs:
                        for dc in range(DC):
                            k.op("pe", [wl, xn], [ps], lambda e_, dc=dc, a=a, w=w, ps=ps, wl=wl: e_.matmul(
                                ps[0:16, a:a + w], lhsT=wl[:, dc, :], rhs=xn[:, dc, a:a + w],
                                start=(dc == 0), stop=(dc == DC - 1)))
                    lr = lr_r.next()
                    k.op("act", [ps], [lr], lambda e_, ps=ps, lr=lr: e_.copy(out=lr[:, :n], in_=ps[0:16, :n]))
                    od = e * 2 + d_
                    for fc in range(4):
                        ps = c.ps2.next()
                        for (a, w) in subs:
                            k.op("pe", [lr, c.wgu], [ps], lambda e_, fc=fc, a=a, w=w, ps=ps, lr=lr, od=od: e_.matmul(
                                ps[:, a:a + w], lhsT=c.wgu[:, od, fc * 128:(fc + 1) * 128], rhs=lr[:, a:a + w],
                                start=True, stop=True))
                        col = od * 4 + fc
                        ex = tmp_r.next()
                        k.op("act", [ps], [ex], lambda e_, ps=ps, ex=ex, col=col: e_.activation(
                            out=ex[:, :n], in_=ps[:, :n], func=AF.Exp, scale=-1.0, bias=c.nb_gate[:, col:col + 1]))
                        ln = tmp_r.next()
                        k.op("act", [ex], [ln], lambda e_, ex=ex, ln=ln: e_.activation(
                            out=ln[:, :n], in_=ex[:, :n], func=AF.Ln, bias=1.0))
                        gs = stf_r.next()
                        k.op("dve", [ln], [gs], lambda e_, ln=ln, gs=gs: e_.tensor_scalar(
                            out=gs[:, :n], in0=ln[:, :n], scalar1=-1.0 / 16.0, scalar2=None, op0=ALU.mult))
                        store_fm(c.gfm, d_ * 4 + fc, gs, t0, n)
            if ti + 1 < len(tiles_a):
                xn_next = prep(ti + 1)
            for (col0, gidx) in tm_groups:
                wt = load_w(col0, 512)
                st = sttm_r.next()
                nb = n // 128
                for tb in range(nb):
                    p, off = ps1_tile(c)
                    for dc in range(DC):
                        k.op("pe", [wt, xn], [p], lambda e_, dc=dc, tb=tb, p=p, off=off, wt=wt: e_.matmul(
                            p[:, off:off + 512], lhsT=xn[:, dc, tb * 128:(tb + 1) * 128], rhs=wt[:, dc, :],
                            start=(dc == 0), stop=(dc == DC - 1)))
                    if tb % 2 == 0:
                        k.op("act", [p], [st], lambda e_, p=p, off=off, st=st, tb=tb: e_.copy(
                            out=st[:, tb, :], in_=p[:, off:off + 512]))
                    else:
                        k.op("dve", [p], [st], lambda e_, p=p, off=off, st=st, tb=tb: e_.tensor_copy(
                            out=st[:, tb, :], in_=p[:, off:off + 512]))
                k.dma(c.utm.ap()[gidx, t0:t0 + n, :].rearrange("(b p) f -> p b f", p=128), st[:, :nb, :],
                      [st], [c.utm.sel([gidx], t0, t0 + n)], q="act")


def phase_B(c, l):
    if l % 2 == 0:
        e = l // 2
        if "gla" in c.cfg.parts:
            gla(c, dv=128, q0=0, kf0=4, kb0=8, gate0=12, v_of=lambda h: (0, h * 128),
                gain=c.g_hg[:, e * 4:(e + 1) * 4], mix0=0)
        if "attn" in c.cfg.parts:
            attention(c)
    else:
        o = l // 2
        gla(c, dv=256, q0=0, kf0=4, kb0=4, gate0=8, v_of=lambda h: (h // 2, (h % 2) * 256),
            gain=c.g_gla[:, o * 8:(o + 1) * 8], mix0=0)


import os as _os0
_GSTOP = int(_os0.environ.get('GLA_STOP', '99'))


def gla(c, dv, q0, kf0, kb0, gate0, v_of, gain, mix0):
    k = c.k
    cfg = c.cfg
    dvc = dv // 128
    G = 512
    ufm, gfm, utm = c.ufm.ap(), c.gfm.ap(), c.utm.ap()
    with Scope(c):
        gla_constants(c)
        TMAX = max(cfg.seqs)
        oacc_r = sb_ring(k, "g_oacc", [128, dvc, TMAX], F32, 2)
        q_r = sb_ring(k, "g_q", [128, G], BF16, 5)
        kk_r = sb_ring(k, "g_k", [128, G], BF16, 5)
        g_r = sb_ring(k, "g_g", [128, G], F32, 5)
        v_r = sb_ring(k, "g_v", [64, 8, dv], BF16, 5)
        b_r = sb_ring(k, "g_b", [128, G], F32, 2)
        d_r = sb_ring(k, "g_d", [128, G], F32, 2)
        eq_r = sb_ring(k, "g_eq", [128, G], F32, 2)
        ek_r = sb_ring(k, "g_ek", [128, G], F32, 2)
        qt_r = sb_ring(k, "g_qt", [128, G], BF16, 5)
        kt_r = sb_ring(k, "g_kt", [128, G], BF16, 5)
        ktm_r = sb_ring(k, "g_ktm", [64, 8, 128], BF16, 5)
        am_r = sb_ring(k, "g_am", [64, 8, 64], BF16, 5)
        c1_r = sb_ring(k, "g_c1", [128, 8], F32, 5)
        Sf2 = [[k.sb("g_Sf%d_%d" % (i, j), [128, dv], F32) for j in range(2)] for i in range(2)]
        Scur = [0, 0]
        sb_r = [sb_ring(k, "g_Sb%d_" % i, [128, dv], BF16, 3) for i in range(2)]
        Sb = [None, None]
        br_r = sb_ring(k, "g_br", [128, G], F32, 2)
        eh_r = sb_ring(k, "g_eh", [128, G], F32, 2)
        qh_r = sb_ring(k, "g_qh", [128, G], BF16, 5)
        sq_r = sb_ring(k, "g_sq", [128, dvc, G], BF16, 2)
        rstd_r = sb_ring(k, "g_rstd", [128, G], F32, 2)
        gate_r = sb_ring(k, "g_gate", [128, dvc, G], BF16, 2)
        on_r = sb_ring(k, "g_on", [128, G], F32, 2)
        out_r = sb_ring(k, "g_out", [128, dvc, G], BF16, 2)

        def g_front(s0, h, d_, gi, lk=None):
            t0 = s0 + gi * G
            kc = (kf0 if d_ == 0 else kb0) + h
            q = q_r.next()
            k.dma(q[:], ufm[q0 + h, :, t0:t0 + G], [c.ufm.sel([q0 + h], t0, t0 + G)], [q])
            kk = kk_r.next()
            k.dma(kk[:], ufm[kc, :, t0:t0 + G], [c.ufm.sel([kc], t0, t0 + G)], [kk])
            g = g_r.next()
            gch = d_ * 4 + h
            k.dma(g[:], gfm[gch, :, t0:t0 + G], [c.gfm.sel([gch], t0, t0 + G)], [g])
            v = v_r.next()
            vg, vc0 = v_of(h)
            k.dma(v[:], utm[vg, t0:t0 + G, vc0:vc0 + dv].rearrange("(b p) f -> p b f", p=64),
                  [c.utm.sel([vg], t0, t0 + G)], [v])
            yield
            b = b_r.next()
            k.op("dve", [c.rmask, g], [b], lambda e: e.tensor_tensor_scan(
                out=b[:], data0=c.rmask[:], data1=g[:], initial=0.0, op0=ALU.mult, op1=ALU.add))
            b3 = b[:].rearrange("p (c j) -> p c j", j=64)
            tot_b = b3[:, :, 63:64].broadcast_to([128, 8, 64])
            c1 = c1_r.next()
            k.op("act", [b], [c1], lambda e: e.activation(out=c1[:].rearrange("p (c o) -> p c o", o=1),
                                                           in_=b3[:, :, 63:64], func=AF.Exp))
            yield
            d = d_r.next()
            d3 = d[:].rearrange("p (c j) -> p c j", j=64)
            if d_ == 0:
                k.op("dve", [b], [d], lambda e: e.tensor_tensor(out=d3, in0=b3, in1=tot_b, op=ALU.subtract))
                bq = b
            else:
                k.op("dve", [b, g], [d], lambda e: e.tensor_tensor(out=d[:], in0=g[:], in1=b[:], op=ALU.subtract))
                bq = br_r.next()
                k.op("dve", [b, d], [bq], lambda e: e.tensor_tensor(
                    out=bq[:].rearrange("p (c j) -> p c j", j=64), in0=d3, in1=tot_b, op=ALU.add))
            yield
            eq = eq_r.next()
            k.op("act", [d], [eq], lambda e: e.activation(out=eq[:], in_=d[:], func=AF.Exp))
            ek = ek_r.next()
            k.op("act", [d], [ek], lambda e: e.activation(out=ek[:], in_=d[:], func=AF.Exp, scale=-1.0))
            eh_ = eh_r.next()
            k.op("act", [bq], [eh_], lambda e: e.activation(out=eh_[:], in_=bq[:], func=AF.Exp))
            yield
            kt = kt_r.next()
            k.op("pool", [kk, ek], [kt], lambda e: e.tensor_tensor(out=kt[:], in0=kk[:], in1=ek[:], op=ALU.mult))
            yield
            qt = qt_r.next()
            k.op("pool", [q, eq], [qt], lambda e: e.tensor_tensor(out=qt[:], in0=q[:], in1=eq[:], op=ALU.mult))
            yield
            qh = qh_r.next()
            k.op("pool", [q, eh_], [qh], lambda e: e.tensor_tensor(out=qh[:], in0=q[:], in1=eh_[:], op=ALU.mult))
            for _ in range(4):
                yield
            yield
            pT = c.psT.next()
            for ch in range(8):
                k.op("pe", [kt, c.ident_b], [pT], lambda e, ch=ch: e.transpose(
                    pT[0:64, ch * 128:(ch + 1) * 128], kt[:, ch * 64:(ch + 1) * 64], c.ident_b[:]))
            yield
            yield
            ktm = ktm_r.next()
            k.op("act", [pT], [ktm], lambda e: e.copy(out=ktm[:], in_=pT[0:64, :].rearrange("p (a f) -> p a f", a=8)))
            yield
            pA, offA = ps1_tile(c)
            for ch in range(8):
                k.op("pe", [kt, qt], [pA], lambda e, ch=ch: e.matmul(
                    pA[0:64, offA + ch * 64:offA + (ch + 1) * 64], lhsT=kt[:, ch * 64:(ch + 1) * 64],
                    rhs=qt[:, ch * 64:(ch + 1) * 64], start=True, stop=True))
            yield
            am = am_r.next()
            mask = c.maskF if d_ == 0 else c.maskB
            k.op("dve", [pA, mask], [am], lambda e: e.tensor_tensor(
                out=am[:], in0=pA[0:64, offA:offA + 512].rearrange("p (a t) -> p a t", a=8),
                in1=mask[0:64, 0:64].rearrange("p (o t) -> p o t", o=1).broadcast_to([64, 8, 64]), op=ALU.mult))
            yield
            cross = lk is not None and ((d_ == 0 and gi * G == lk) or (d_ == 1 and (gi + 1) * G == lk))
            return dict(d_=d_, gi=gi, v=v, ktm=ktm, qh=qh, c1=c1, am=am, cross=cross)

        def g_front2(st):
            d_, v, am = st["d_"], st["v"], st["am"]
            pO, offO = c.ps2.next(), 0
            for ch in range(8):
                for eh in range(dvc):
                    k.op("pe", [v, am], [pO], lambda e, ch=ch, eh=eh: e.matmul(
                        pO[:, offO + eh * 512 + ch * 64:offO + eh * 512 + (ch + 1) * 64],
                        lhsT=v[:, ch, eh * 128:(eh + 1) * 128], rhs=am[:, ch, :], start=(ch == 0), stop=False,
                        skip_group_check=True))
            cper = 512 // dv
            order = list(range(8)) if d_ == 0 else list(range(7, -1, -1))
            st.update(pO=pO, offO=offO, cper=cper, order=order, ps_=None)

        def g_step(st, oi):
            d_, v, ktm, qh, c1, pO, offO, cper, order = (st[x] for x in
                                                        ("d_", "v", "ktm", "qh", "c1", "pO", "offO", "cper", "order"))
            S = Sf2[d_][Scur[d_]]
            Sn = Sf2[d_][1 - Scur[d_]]
            ch = order[oi]
            if oi == 0 and st["cross"]:
                k.op("dve", [S, c.link], [S], lambda e: e.tensor_scalar(
                    out=S[:], in0=S[:], scalar1=c.link[:, 0:1], scalar2=None, op0=ALU.mult))
                nsb0 = sb_r[d_].next()
                k.op("act", [S], [nsb0], lambda e, nsb0=nsb0: e.copy(out=nsb0[:], in_=S[:]))
                Sb[d_] = nsb0
            if oi % cper == 0:
                st["ps_"] = ps1_tile(c)
                ps_, pso = st["ps_"]
                for ch2 in order[oi:oi + cper]:
                    col2 = pso + (ch2 % cper) * dv
                    k.op("pe", [ktm, v], [ps_], lambda e, ch2=ch2, ps_=ps_, col2=col2: e.matmul(
                        ps_[:, col2:col2 + dv], lhsT=ktm[:, ch2, :], rhs=v[:, ch2, :], start=True, stop=True))
            ps_, pso = st["ps_"]
            col = pso + (ch % cper) * dv
            sbf = Sb[d_]
            for eh in range(dvc):
                k.op("pe", [sbf, qh], [pO], lambda e, ch=ch, eh=eh, sbf=sbf: e.matmul(
                    pO[:, offO + eh * 512 + ch * 64:offO + eh * 512 + (ch + 1) * 64],
                    lhsT=sbf[:, eh * 128:(eh + 1) * 128], rhs=qh[:, ch * 64:(ch + 1) * 64], start=False, stop=True,
                    skip_group_check=True))
            k.op("dve", [S, c1, ps_], [Sn], lambda e, ch=ch, ps_=ps_, col=col: e.scalar_tensor_tensor(
                out=Sn[:], in0=S[:], scalar=c1[:, ch:ch + 1], in1=ps_[:, col:col + dv], op0=ALU.mult, op1=ALU.add))
            nsb = sb_r[d_].next()
            k.op("act", [Sn], [nsb], lambda e, nsb=nsb: e.copy(out=nsb[:], in_=Sn[:]))
            Sb[d_] = nsb
            Scur[d_] = 1 - Scur[d_]

        def g_finish(st, oacc, first):
            pO, offO, gi = st["pO"], st["offO"], st["gi"]
            src = pO[:, offO:offO + dvc * 512].rearrange("p (a t) -> p a t", a=dvc)
            dst = oacc[:, :, gi * G:(gi + 1) * G]
            if first:
                k.op("act", [pO], [oacc], lambda e: e.copy(out=dst, in_=src))
            else:
                k.op("dve", [pO, oacc], [oacc], lambda e: e.tensor_tensor(out=dst, in0=dst, in1=src, op=ALU.add))

        def run_gen(gen, nsteps=None):
            try:
                n = 0
                while nsteps is None or n < nsteps:
                    next(gen)
                    n += 1
            except StopIteration as stop:
                return stop.value
            return None

        def finalize(s0, h, ng, oacc):
            for gi in range(ng):
                t0 = s0 + gi * G
                osl = Buf(oacc.t, "oview", oacc.leaves)
                sq = sq_r.next()
                k.op("act", [oacc], [sq], lambda e, sq=sq, gi=gi: e.activation(
                    out=sq[:], in_=oacc[:, :, gi * G:(gi + 1) * G], func=AF.Square))
                p, off = ps1_tile(c)
                for eh in range(dvc):
                    k.op("pe", [sq, c.ones_b], [p], lambda e, eh=eh, p=p, off=off, sq=sq: e.matmul(
                        p[:, off:off + G], lhsT=c.ones_b[:], rhs=sq[:, eh, :], start=(eh == 0), stop=(eh == dvc - 1)))
                rstd = rstd_r.next()
                k.op("act", [p], [rstd], lambda e, p=p, off=off, rstd=rstd: e.activation(
                    out=rstd[:], in_=p[:, off:off + G], func=AF.Ln, scale=1.0 / dv, bias=EPS))
                k.op("act", [rstd], [rstd], lambda e, rstd=rstd: e.activation(
                    out=rstd[:], in_=rstd[:], func=AF.Exp, scale=-0.5))
                gate = gate_r.next()
                outb = out_r.next()
                for eh in range(dvc):
                    ch = gate0 + h * dvc + eh
                    k.dma(gate[:, eh, :], ufm[ch, :, t0:t0 + G], [c.ufm.sel([ch], t0, t0 + G)], [gate])
                for eh in range(dvc):
                    on = on_r.next()
                    k.op("pool", [oacc, rstd], [on], lambda e, on=on, eh=eh, gi=gi, rstd=rstd: e.tensor_tensor(
                        out=on[:], in0=oacc[:, eh, gi * G:(gi + 1) * G], in1=rstd[:], op=ALU.mult))
                    gcol = h * dvc + eh
                    k.op("dve", [on, gate], [outb], lambda e, on=on, eh=eh, gate=gate, outb=outb, gcol=gcol:
                         e.scalar_tensor_tensor(out=outb[:, eh, :], in0=on[:], scalar=gain[:, gcol:gcol + 1],
                                                in1=gate[:, eh, :], op0=ALU.mult, op1=ALU.mult))
                    mch = mix0 + h * dvc + eh
                    k.dma(c.mix.ap()[mch, :, t0:t0 + G], outb[:, eh, :], [outb], [c.mix.sel([mch], t0, t0 + G)], q="act")

        pairs = []
        for si, (s0, T) in enumerate(zip(cfg.starts, cfg.seqs)):
            ng = T // G
            for h in range(4):
                for i in range(ng):
                    pairs.append(dict(s0=s0, h=h, i=i, ng=ng, lk=cfg.links.get(si)))

        def fronts_of(P):
            return [g_front(P["s0"], P["h"], d_, gi, P["lk"]) for d_, gi in ((0, P["i"]), (1, P["ng"] - 1 - P["i"]))]

        cur = [run_gen(g) for g in fronts_of(pairs[0])]
        oacc = None
        seen_g = set()
        for n, P in enumerate(pairs):
            if P["i"] == 0:
                oacc = oacc_r.next()
                seen_g = set()
                for d_ in range(2):
                    k.op("pool", [], [Sf2[d_][Scur[d_]]], lambda e, d_=d_: e.memset(Sf2[d_][Scur[d_]][:], 0.0))
                    Sb[d_] = sb_r[d_].next()
                    k.op("pool", [], [Sb[d_]], lambda e, d_=d_: e.memset(Sb[d_][:], 0.0))
            chains = cur
            for st in chains:
                g_front2(st)
            nxt_gens = fronts_of(pairs[n + 1]) if n + 1 < len(pairs) else []
            nxt = []
            for oi in range(8):
                for st in chains:
                    g_step(st, oi)
                if len(nxt) < len(nxt_gens):
                    r_ = run_gen(nxt_gens[len(nxt)], 4)
                    if r_ is not None:
                        nxt.append(r_)
            while len(nxt) < len(nxt_gens):
                nxt.append(run_gen(nxt_gens[len(nxt)]))
            for st in chains:
                g_finish(st, oacc, st["gi"] not in seen_g)
                seen_g.add(st["gi"])
            if P["i"] == P["ng"] - 1:
                finalize(P["s0"], P["h"], P["ng"], oacc)
            cur = nxt


def attention(c):
    k = c.k
    cfg = c.cfg
    ufm, utm = c.ufm.ap(), c.utm.ap()
    with Scope(c):
        attn_constants(c)
        TMAX = max(cfg.seqs)
        acc_r = sb_ring(k, "t_acc", [128, 2, TMAX], F32, 1)
        qn_r = sb_ring(k, "t_qn", [128, TMAX], BF16, 2)
        kn_r = sb_ring(k, "t_kn", [128, TMAX], BF16, 2)
        qd_r = sb_ring(k, "t_qd", [128, TMAX], BF16, 2)
        kd_r = sb_ring(k, "t_kd", [128, TMAX], BF16, 2)
        NBLK = TMAX // 128 + 16
        vb_r = Ring([k.sb("t_vb%d" % i, [128, NBLK, 128], BF16).with_parts(NBLK) for i in range(2)])
        pb_r = sb_ring(k, "t_pb", [128, 256], BF16, 6)
        rec_r = sb_ring(k, "t_rec", [128, TMAX], F32, 1)
        ob_r = sb_ring(k, "t_ob", [128, TMAX], BF16, 2)
        for vb in vb_r.bufs:
            k.op("pool", [], [vb], lambda e, vb=vb: e.memset(vb[:], 0.0))
        for si, (s0, T) in enumerate(zip(cfg.starts, cfg.seqs)):
            lk = cfg.links.get(si)
            for h in range(4):
                acc = acc_r.next()
                pending = []
                for g, dil in enumerate(B_GROUPS_DIL):
                    L = T // dil
                    nb = L // 128
                    assert L % 128 == 0
                    bidx = g * 4 + h
                    qc, kc = 16 + g * 8 + h, 20 + g * 8 + h
                    qn = qn_r.next()
                    k.dma(qn[:, :T], ufm[qc, :, s0:s0 + T], [c.ufm.sel([qc], s0, s0 + T)], [qn])
                    kn = kn_r.next()
                    k.dma(kn[:, :T], ufm[kc, :, s0:s0 + T], [c.ufm.sel([kc], s0, s0 + T)], [kn])
                    if dil == 1:
                        qd, kd = qn, kn
                    else:
                        qd = qd_r.next()
                        k.op("pool", [qn], [qd], lambda e, qd=qd, qn=qn, dil=dil, L=L: e.tensor_copy(
                            out=qd[:, :T].rearrange("p (r m) -> p r m", r=dil),
                            in_=qn[:, :T].rearrange("p (m r) -> p r m", r=dil)))
                        kd = kd_r.next()
                        k.op("pool", [kn], [kd], lambda e, kd=kd, kn=kn, dil=dil, L=L: e.tensor_copy(
                            out=kd[:, :T].rearrange("p (r m) -> p r m", r=dil),
                            in_=kn[:, :T].rearrange("p (m r) -> p r m", r=dil)))
                    vb = vb_r.next()
                    vsrc = utm[1 + g, s0:s0 + T, h * 128:(h + 1) * 128].rearrange("(m r) f -> r m f", r=dil)
                    vrd = [c.utm.sel([1 + g], s0, s0 + T)]
                    for r in range(dil):
                        b0 = r * (nb + 1)
                        k.dma(vb[0:64, b0, :], vsrc[r, 0:64, :], vrd, [vb.part(b0)])
                        k.dma(vb[0:64, b0 + nb, :], vsrc[r, L - 64:L, :], vrd, [vb.part(b0 + nb)])
                        if nb > 1:
                            k.dma(vb[:, b0 + 1:b0 + nb, :],
                                  vsrc[r, 64:64 + 128 * (nb - 1), :].rearrange("(i p) f -> p i f", p=128), vrd,
                                  [Buf(vb.t, "vbi", vb.leaves[b0 + 1:b0 + nb])])
                    accv = acc[:, :, :T].rearrange("p a (m r) -> p a r m", r=dil)
                    for r in range(dil):
                        base = r * L
                        b0 = r * (nb + 1)
                        prev = None
                        for i in range(nb + 1):
                            pS, offS = ps1_tile(c)
                            pb = pb_r.next()
                            if i == 0:
                                kr, c0, c1_ = 64, 128, 256
                                kcols = (base, base + 64)
                                qcols = (base, base + 128)
                                idl = c.ident_b[0:64, 0:64]
                                bh = c.bias0_hi[:, bidx, :]
                                bl = c.bias0_lo[:, bidx, :]
                            elif i == nb:
                                kr, c0, c1_ = 64, 0, 128
                                kcols = (base + L - 64, base + L)
                                qcols = (base + 128 * (nb - 1), base + 128 * nb)
                                idl = c.ident_b[0:64, 0:64]
                                bh = c.bias_hi[0:64, bidx, 0:128]
                                bl = c.bias_lo[0:64, bidx, 0:128]
                            else:
                                kr, c0, c1_ = 128, 0, 256
                                kcols = (base + 128 * i - 64, base + 128 * i + 64)
                                qcols = (base + 128 * (i - 1), base + 128 * (i + 1))
                                idl = c.ident_b[:, :]
                                bh = c.bias_hi[:, bidx, :]
                                bl = c.bias_lo[:, bidx, :]
                            out = pS[0:kr, offS + c0:offS + c1_]
                            k.op("pe", [kd, qd], [pS], lambda e, out=out, kcols=kcols, qcols=qcols, kd=kd, qd=qd: e.matmul(
                                out, lhsT=kd[:, kcols[0]:kcols[1]], rhs=qd[:, qcols[0]:qcols[1]], start=True, stop=False))
                            k.op("pe", [c.bias_hi, c.bias0_hi], [pS], lambda e, out=out, idl=idl, bh=bh: e.matmul(
                                out, lhsT=idl, rhs=bh, start=False, stop=False))
                            k.op("pe", [c.bias_lo, c.bias0_lo], [pS], lambda e, out=out, idl=idl, bl=bl: e.matmul(
                                out, lhsT=idl, rhs=bl, start=False, stop=True))
                            k.op("act", [pS], [pb], lambda e, out=out, pb=pb, kr=kr, c0=c0, c1_=c1_: e.activation(
                                out=pb[0:kr, c0:c1_], in_=out, func=AF.Exp))
                            if lk is not None and 128 * i == lk // dil:
                                assert 0 < i < nb
                                k.op("dve", [pb, c.link], [pb], lambda e, pb=pb: e.tensor_scalar(
                                    out=pb[0:64, 128:256], in0=pb[0:64, 128:256], scalar1=c.link[0:64, 0:1],
                                    scalar2=None, op0=ALU.mult))
                                k.op("dve", [pb, c.link], [pb], lambda e, pb=pb: e.tensor_scalar(
                                    out=pb[64:128, 0:128], in0=pb[64:128, 0:128], scalar1=c.link[64:128, 0:1],
                                    scalar2=None, op0=ALU.mult))
                            if prev is not None:
                                def pv_job(j=i - 1, ppb=prev[0], pkr=prev[1], pb=pb, kr=kr, b0=b0, r=r, i=i, g=g,
                                           vb=vb, accv=accv, acc=acc):
                                    pO, offO = ps1_tile(c)
                                    for (lhs, dst) in ((None, 0), ("ones", 128)):
                                        for (pbuf, krr, blk, cs, st_, sp_) in ((ppb, pkr, b0 + j, 128, True, False),
                                                                                (pb, kr, b0 + i, 0, False, True)):
                                            if lhs is None:
                                                lt = vb[0:krr, blk, :]
                                                rd = [vb.part(blk), pbuf]
                                            else:
                                                lt = c.ones_b[0:krr, :]
                                                rd = [pbuf]
                                            k.op("pe", rd, [pO], lambda e, lt=lt, pbuf=pbuf, krr=krr, cs=cs, dst=dst,
                                                 st_=st_, sp_=sp_, pO=pO, offO=offO: e.matmul(
                                                pO[:, offO + dst:offO + dst + 128], lhsT=lt, rhs=pbuf[0:krr, cs:cs + 128],
                                                start=st_, stop=sp_))
                                    src = pO[:, offO:offO + 256].rearrange("p (a t) -> p a t", a=2)
                                    dstv = accv[:, :, r, 128 * j:128 * (j + 1)]
                                    if g == 0:
                                        k.op("act", [pO], [acc], lambda e, src=src, dstv=dstv: e.copy(out=dstv, in_=src))
                                    else:
                                        k.op("dve", [pO, acc], [acc], lambda e, src=src, dstv=dstv: e.tensor_tensor(
                                            out=dstv, in0=dstv, in1=src, op=ALU.add))
                                pending.append(pv_job)
                            while len(pending) > 1:
                                pending.pop(0)()
                            prev = (pb, kr)
                while pending:
                    pending.pop(0)()
                rec = rec_r.next()
                k.op("dve", [acc], [rec], lambda e, rec=rec, acc=acc: e.reciprocal(out=rec[:, :T], in_=acc[:, 1, :T]))
                ob = ob_r.next()
                k.op("pool", [acc, rec], [ob], lambda e, rec=rec, acc=acc, ob=ob: e.tensor_tensor(
                    out=ob[:, :T], in0=acc[:, 0, :T], in1=rec[:, :T], op=ALU.mult))
                k.dma(c.mix.ap()[4 + h, :, s0:s0 + T], ob[:, :T], [ob], [c.mix.sel([4 + h], s0, s0 + T)], q="act")


def windows_of(cfg, wmax):
    out = []
    for s0, T in zip(cfg.starts, cfg.seqs):
        nwin = 1
        while True:
            per = -(-T // nwin)
            per = -(-per // 128) * 128
            if per + 2 <= wmax or (nwin == 1 and T <= wmax):
                break
            nwin += 1
        a = 0
        while a < T:
            b = min(T, a + per)
            wa, wb = max(0, a - 1), min(T, b + 1)
            out.append((s0, T, s0 + a, s0 + b, s0 + wa, s0 + wb))
            a = b
    return out


def phase_C(c, l):
    k = c.k
    cfg = c.cfg
    even = (l % 2 == 0)
    e = l // 2
    Wo = c.wb["ev_w_out" if even else "od_w_out"].part(e)
    Wup, Wdn, Wpg, Wpp = (c.wb["ffn_w_up"].part(l), c.wb["ffn_w_down"].part(l), c.wb["ple_w_gate"].part(l),
                          c.wb["ple_w_proj"].part(l))
    WM = 1024
    h_ap = c.h_d.ap().rearrange("c p t -> p c t")
    mix_ap = c.mix.ap().rearrange("c p t -> p c t")
    hn_ap = c.h_n.ap().rearrange("c p t -> p c t")
    p_ap = c.ext["p"].ap()
    cw = c.conv_w[:].rearrange("p (l j f) -> p l j f", l=DEPTH, j=3)
    cwn = c.conv_w_nl[:].rearrange("p (l j f) -> p l j f", l=DEPTH, j=3)
    bounds = [s0_ + cfg.links[si] for si, s0_ in enumerate(cfg.starts) if si in cfg.links]
    cb = c.conv_b[:].rearrange("p (l f) -> p l f", l=DEPTH)
    with Scope(c):
        h_r = sb_ring(k, "c_h", [128, DC, WM], F32, 1)
        actbuf = k.sb("c_act", [128, NFF, WM], BF16).with_parts(NFF)
        sq_v = Buf(actbuf.t, "c_sq", actbuf.leaves[0:DC])
        mx_v = Buf(actbuf.t[:, DC:2 * DC, :], "c_mx", actbuf.leaves[DC:2 * DC])
        sq_r = Ring([sq_v])
        mx_r = Ring([mx_v])
        rstd_r = sb_ring(k, "c_rstd", [128, WM], F32, 1)
        xn_r = sb_ring(k, "c_xn", [128, DC, WM], BF16, 1)
        w_r = sb_ring(k, "c_w", [128, DC, 512], BF16, 4)
        wd_r = sb_ring(k, "c_wd", [128, NFF, 128], BF16, 2)
        wp_r = sb_ring(k, "c_wp", [128, 2, 512], BF16, 2)
        u_r = sb_ring(k, "c_u", [128, WM], F32, 2)
        ca_r = sb_ring(k, "c_ca", [128, WM], F32, 2)
        cg_r = sb_ring(k, "c_cg", [128, WM], F32, 2)
        sg_r = sb_ring(k, "c_sg", [128, WM], F32, 2)
        pin_r = sb_ring(k, "c_pin", [128, PLE], F32, 3)
        pT_r = sb_ring(k, "c_pT", [128, 2, WM], BF16, 1)

        def load_w(Wb, r0, col0, w):
            wt = w_r.next()
            k.dma(wt[:, :, :w], Wb.ap()[r0:r0 + D, col0:col0 + w].rearrange("(c p) f -> p c f", p=128), [Wb], [wt])
            return wt

        for (s0, T, a, b, wa, wb) in windows_of(cfg, WM):
            nw = wb - wa
            no = b - a
            oo = a - wa
            mx = mx_r.next()
            k.dma(mx[:, :, :nw], mix_ap[:, :, wa:wb], [c.mix.sel(range(8), wa, wb)], [mx])
            wt_first = load_w(Wo, e * D, 0, 512)
            ht = h_r.next()
            k.dma(ht[:, :, :nw], h_ap[:, :, wa:wb], [c.h_d.sel([0], wa, wb)], [ht])
            subs = subtiles(nw)
            osubs = [(oo + a_, w_) for (a_, w_) in subtiles(no)]
            for ob in range(2):
                wt = wt_first if ob == 0 else load_w(Wo, e * D, ob * 512, 512)
                for fc in range(4):
                    oc = ob * 4 + fc
                    ps = c.ps2.next()
                    for (sa, sw) in subs:
                        for dc in range(DC):
                            k.op("pe", [wt, mx], [ps], lambda e_, fc=fc, dc=dc, sa=sa, sw=sw, ps=ps, wt=wt: e_.matmul(
                                ps[:, sa:sa + sw], lhsT=wt[:, dc, fc * 128:(fc + 1) * 128], rhs=mx[:, dc, sa:sa + sw],
                                start=(dc == 0), stop=(dc == DC - 1)))
                    k.op("dve", [ps, ht], [ht], lambda e_, oc=oc, ps=ps: e_.tensor_tensor(
                        out=ht[:, oc, :nw], in0=ht[:, oc, :nw], in1=ps[:, :nw], op=ALU.add))
            xn = xn_r.next()
            norm_xn(c, ht, nw, c.g_ffn[:, l * DC:(l + 1) * DC], sq_r, rstd_r, xn)
            actT = actbuf
            for j in range(NFF):
                if j % 4 == 0:
                    nj = min(4, NFF - j)
                    wa_t = load_w(Wup, l * D, j * 128, nj * 128)
                    wg_t = load_w(Wup, l * D, D_FF + j * 128, nj * 128)
                jj = j % 4
                res = []
                for (wt, fidx, ring) in ((wg_t, NFF + j, cg_r), (wa_t, j, ca_r)):
                    pp = c.ps2.next()
                    for (sa, sw) in subs:
                        for dc in range(DC):
                            k.op("pe", [wt, xn], [pp], lambda e_, jj=jj, dc=dc, sa=sa, sw=sw, pp=pp, wt=wt: e_.matmul(
                                pp[:, sa:sa + sw], lhsT=wt[:, dc, jj * 128:(jj + 1) * 128], rhs=xn[:, dc, sa:sa + sw],
                                start=(dc == 0), stop=(dc == DC - 1)))
                    us = u_r.next()
                    k.op("act", [pp], [us], lambda e_, pp=pp, us=us: e_.copy(out=us[:, :nw], in_=pp[:, :nw]))
                    cc = ring.next()
                    k.op("act", [us], [cc], lambda e_, us=us, cc=cc, fidx=fidx: e_.activation(
                        out=cc[:, :nw], in_=us[:, :nw], func=AF.Identity, scale=cw[:, l, 1, fidx:fidx + 1],
                        bias=cb[:, l, fidx:fidx + 1]))
                    k.op("dve", [us, cc], [cc], lambda e_, us=us, cc=cc, fidx=fidx: e_.scalar_tensor_tensor(
                        out=cc[:, 1:nw], in0=us[:, 0:nw - 1], scalar=cw[:, l, 0, fidx:fidx + 1], in1=cc[:, 1:nw],
                        op0=ALU.mult, op1=ALU.add))
                    k.op("dve", [us, cc], [cc], lambda e_, us=us, cc=cc, fidx=fidx: e_.scalar_tensor_tensor(
                        out=cc[:, 0:nw - 1], in0=us[:, 1:nw], scalar=cw[:, l, 2, fidx:fidx + 1], in1=cc[:, 0:nw - 1],
                        op0=ALU.mult, op1=ALU.add))
                    for B_ in bounds:
                        if wa <= B_ - 1 and B_ <= wb - 1:
                            lb = B_ - wa
                            k.op("dve", [us, cc], [cc], lambda e_, us=us, cc=cc, fidx=fidx, lb=lb: e_.scalar_tensor_tensor(
                                out=cc[:, lb:lb + 1], in0=us[:, lb - 1:lb], scalar=cwn[:, l, 0, fidx:fidx + 1],
                                in1=cc[:, lb:lb + 1], op0=ALU.mult, op1=ALU.add))
                            k.op("dve", [us, cc], [cc], lambda e_, us=us, cc=cc, fidx=fidx, lb=lb: e_.scalar_tensor_tensor(
                                out=cc[:, lb - 1:lb], in0=us[:, lb:lb + 1], scalar=cwn[:, l, 2, fidx:fidx + 1],
                                in1=cc[:, lb - 1:lb], op0=ALU.mult, op1=ALU.add))
                    if fidx >= NFF:
                        pend_gelu = cc
                    else:
                        k.op("act", [pend_gelu], [pend_gelu], lambda e_, cc=pend_gelu: e_.activation(
                            out=cc[:, :nw], in_=cc[:, :nw], func=AF.Gelu))
                    res.append(cc)
                cgt, cat = res
                k.op("pool", [cat, cgt], [actT.part(j)], lambda e_, cat=cat, cgt=cgt, j=j: e_.tensor_tensor(
                    out=actT[:, j, :nw], in0=cat[:, :nw], in1=cgt[:, :nw], op=ALU.mult))
            for oc in range(DC):
                wd = wd_r.next()
                k.dma(wd[:], Wdn.ap()[l * D_FF:(l + 1) * D_FF, oc * 128:(oc + 1) * 128].rearrange("(j p) f -> p j f", p=128),
                      [Wdn], [wd])
                ps = c.ps2.next()
                for (sa, sw) in osubs:
                    for j in range(NFF):
                        k.op("pe", [wd, actT], [ps], lambda e_, j=j, sa=sa, sw=sw, ps=ps, wd=wd: e_.matmul(
                            ps[:, sa - oo:sa - oo + sw], lhsT=wd[:, j, :], rhs=actT[:, j, sa:sa + sw],
                            start=(j == 0), stop=(j == NFF - 1)))
                k.op("dve", [ps, ht], [ht], lambda e_, oc=oc, ps=ps: e_.tensor_tensor(
                    out=ht[:, oc, oo:oo + no], in0=ht[:, oc, oo:oo + no], in1=ps[:, :no], op=ALU.add))
            hv = Buf(ht.t, "hview", ht.leaves)
            xn2 = xn_r.next()
            rstd = rms_rstd_view(c, ht, oo, no, sq_r, rstd_r)
            for cc_ in range(DC):
                k.op("dve", [ht, rstd], [xn2], lambda e_, cc_=cc_: e_.scalar_tensor_tensor(
                    out=xn2[:, cc_, :no], in0=ht[:, cc_, oo:oo + no], scalar=c.g_ple[:, l * DC + cc_:l * DC + cc_ + 1],
                    in1=rstd[:, :no], op0=ALU.mult, op1=ALU.mult))
            pT = pT_r.next()
            for b0 in range(0, no, 128):
                nbk = min(128, no - b0)
                pin = pin_r.next()
                k.dma(pin[:nbk, :], p_ap[l, a + b0:a + b0 + nbk, :], [c.ext["p"]], [pin])
                pp, off = ps1_tile(c)
                for jj in range(2):
                    k.op("pe", [pin, c.ident_f], [pp], lambda e_, jj=jj, pp=pp, off=off, pin=pin, nbk=nbk: e_.transpose(
                        pp[:, off + jj * 128:off + jj * 128 + nbk], pin[:nbk, jj * 128:(jj + 1) * 128], c.ident_f[:nbk, :nbk]))
                k.op("act", [pp], [pT], lambda e_, pp=pp, off=off, b0=b0, nbk=nbk: e_.copy(
                    out=pT[:, :, b0:b0 + nbk], in_=pp[:, off:off + 256].rearrange("p (j t) -> p j t", j=2)[:, :, :nbk]))
            osub0 = subtiles(no)
            for ob in range(2):
                wt = load_w(Wpg, l * D, ob * 512, 512)
                wp = wp_r.next()
                k.dma(wp[:], Wpp.ap()[l * PLE:(l + 1) * PLE, ob * 512:(ob + 1) * 512].rearrange("(c p) f -> p c f", p=128),
                      [Wpp], [wp])
                for fc in range(4):
                    oc = ob * 4 + fc
                    ps = c.ps2.next()
                    for (sa, sw) in osub0:
                        for dc in range(DC):
                            k.op("pe", [wt, xn2], [ps], lambda e_, fc=fc, dc=dc, sa=sa, sw=sw, ps=ps, wt=wt: e_.matmul(
                                ps[:, sa:sa + sw], lhsT=wt[:, dc, fc * 128:(fc + 1) * 128], rhs=xn2[:, dc, sa:sa + sw],
                                start=(dc == 0), stop=(dc == DC - 1)))
                    sg = sg_r.next()
                    k.op("act", [ps], [sg], lambda e_, ps=ps, sg=sg: e_.activation(out=sg[:, :no], in_=ps[:, :no], func=AF.Sigmoid))
                    ps_p = c.ps2.next()
                    for (sa, sw) in osub0:
                        for dc in range(2):
                            k.op("pe", [wp, pT], [ps_p], lambda e_, fc=fc, dc=dc, sa=sa, sw=sw, ps_p=ps_p, wp=wp: e_.matmul(
                                ps_p[:, sa:sa + sw], lhsT=wp[:, dc, fc * 128:(fc + 1) * 128], rhs=pT[:, dc, sa:sa + sw],
                                start=(dc == 0), stop=(dc == 1)))
                    k.op("dve", [ps_p, sg], [sg], lambda e_, ps_p=ps_p, sg=sg: e_.tensor_tensor(
                        out=sg[:, :no], in0=sg[:, :no], in1=ps_p[:, :no], op=ALU.mult))
                    k.op("pool", [sg, ht], [ht], lambda e_, oc=oc, sg=sg: e_.tensor_tensor(
                        out=ht[:, oc, oo:oo + no], in0=ht[:, oc, oo:oo + no], in1=sg[:, :no], op=ALU.add))
            k.dma(hn_ap[:, :, a:b], ht[:, :, oo:oo + no], [ht], [c.h_n.sel([0], a, b)], q="act")
        c.h_d, c.h_n = c.h_n, c.h_d


def rms_rstd_view(c, hT, o0, n, sq_r, rstd_r):
    k = c.k
    sq = sq_r.next()
    k.op("act", [hT], [sq], lambda e: e.activation(out=sq[:, :DC, :n], in_=hT[:, :, o0:o0 + n], func=AF.Square))
    rstd = rstd_r.next()
    for (a, w) in subtiles(n):
        p, off = ps1_tile(c)
        for cc in range(DC):
            k.op("pe", [sq, c.ones_b], [p], lambda e, cc=cc, p=p, off=off, a=a, w=w: e.matmul(
                p[:, off:off + w], lhsT=c.ones_b[:], rhs=sq[:, cc, a:a + w], start=(cc == 0), stop=(cc == DC - 1)))
        k.op("act", [p], [rstd], lambda e, p=p, off=off, a=a, w=w: e.activation(
            out=rstd[:, a:a + w], in_=p[:, off:off + w], func=AF.Ln, scale=1.0 / D, bias=EPS))
    k.op("act", [rstd], [rstd], lambda e: e.activation(out=rstd[:, :n], in_=rstd[:, :n], func=AF.Exp, scale=-0.5))
    return rstd


def debug_dump(c):
    pass


def add_debug_outputs(c):
    k = c.k
    NT = c.cfg.nt
    for name, src, shape, dt in (("dbg_h", c.h_d, [DC, 128, NT], F32), ("dbg_ufm", c.ufm, [40, 128, NT], BF16),
                                 ("dbg_gfm", c.gfm, [8, 128, NT], F32), ("dbg_utm", c.utm, [4, NT, 512], BF16),
                                 ("dbg_mix", c.mix, [8, 128, NT], BF16)):
        dst = k.dram(name, shape, dt, kind="ExternalOutput")
        for i in range(shape[0]):
            k.dma(dst.ap()[i], src.ap()[i], [src], [dst])
        c.dbg.append(dst)


N_CORES = 8
SEQS_FULL = [4096, 2048]
LINKS_FULL = {0: 2048}
_CACHE = {}


def _core_tokens(x_prompt, x_sample, core):
    if core < 4:
        return np.concatenate([x_prompt[core], x_sample[core]], axis=0)
    j = 4 + 3 * (core - 4)
    return np.concatenate([x_sample[j], x_sample[j + 1], x_sample[j + 2]], axis=0)


def kernel(**inputs):
    cfg = Cfg(SEQS_FULL, links=LINKS_FULL)
    if "nc" not in _CACHE:
        _CACHE["nc"] = build(cfg)
    nc = _CACHE["nc"]
    xp = np.asarray(inputs["x_prompt"], dtype=np.float32)
    xs = np.asarray(inputs["x_sample"], dtype=np.float32)
    pp = np.asarray(inputs["p_prompt"], dtype=np.float32)
    psm = np.asarray(inputs["p_sample"], dtype=np.float32)
    params = {name: np.ascontiguousarray(np.asarray(inputs[name], dtype=np.float32)) for name, _ in PARAM_SHAPES}
    in_maps = []
    for core in range(N_CORES):
        m = dict(params)
        m["x"] = np.ascontiguousarray(_core_tokens(xp, xs, core))
        m["p"] = np.ascontiguousarray(np.stack(
            [_core_tokens(pp[l], psm[l], core) for l in range(DEPTH)], axis=0))
        m["link"] = np.full((128, 1), 1.0 if core < 4 else 0.0, dtype=np.float32)
        in_maps.append(m)
    res = run_bass_kernel_spmd(nc, in_maps, core_ids=list(range(N_CORES)))
    y_prompt = np.empty_like(xp)
    y_sample = np.empty_like(xs)
    for core in range(N_CORES):
        y = res.results[core]["y"]
        if core < 4:
            y_prompt[core] = y[0:4096]
            y_sample[core] = y[4096:6144]
        else:
            j = 4 + 3 * (core - 4)
            for t in range(3):
                y_sample[j + t] = y[2048 * t:2048 * (t + 1)]
    return (y_prompt, y_sample)
```
